# Optimizing a Trainium2 kernel written in Bass

```python
import math
import jax, jax.numpy as jnp
from jax import lax
import numpy as np

D_MODEL = 1024
BATCH = 2
SEQ = 16384
DEPTH = 2
DEC_BATCH = 4
DEC_SEQ = 4096
PAST_LEN = 128

N_BRANCH = 3
BRANCH_WIDTH = 512
NORM_EPS = 1e-6
H_A = 4
DK_A = 128
DV_A = 128
CONV_W = 5
DELTA_CHUNK = 64
QKV_A_COLS = 3 * H_A * DK_A
SG_GROUPS = 4
SG_CHUNK = 128
SG_WIDTH = 512
H_C = 4
Q_LORA = 384
KV_LORA = 256
NOPE_DIM = 128
ROPE_DIM = 64
DV_C = 128
QK_HEAD_DIM = NOPE_DIM + ROPE_DIM
ROPE_THETA = 10000.0
SM_SCALE = QK_HEAD_DIM ** -0.5
Q_BLOCK = 128
PEER_HEADS = 8
N_KEYS = 128
N_EXPERTS = N_KEYS * N_KEYS
PEER_QDIM = 256
PEER_TOPK = 16
PEER_BLOCK = 128
IN_SPLIT_SIZES = (QKV_A_COLS, 2 * H_A, 2 * H_A, H_A * DV_A, SG_WIDTH, SG_WIDTH, Q_LORA, KV_LORA, ROPE_DIM, N_BRANCH * D_MODEL)
IN_COLS = sum(IN_SPLIT_SIZES)

kernel_name = 'hybrid_gdn_sgmlp_mla_peer_encoder'


def rms_norm(x, gain):
    xf = x.astype(jnp.float32)
    y = xf * lax.rsqrt(jnp.mean(xf * xf, axis=-1, keepdims=True) + NORM_EPS)
    return (y * gain.astype(jnp.float32)).astype(x.dtype)


def l2_normalize(x):
    return x * lax.rsqrt(jnp.sum(x * x, axis=-1, keepdims=True) + NORM_EPS)


def centred_depthwise_conv(x, w):
    return lax.conv_general_dilated(x, w[:, None, :], window_strides=(1,), padding=[(CONV_W // 2, CONV_W // 2)],
                                    dimension_numbers=('NWC', 'WIO', 'NWC'), feature_group_count=x.shape[-1])


def chunked_delta_rule(q, k, v, log_alpha, beta):
    B, S, H, DK = q.shape
    DV = v.shape[-1]
    C = DELTA_CHUNK
    n = S // C

    def blocks(t):
        t = t.reshape((B, n, C, H) + t.shape[3:])
        return jnp.swapaxes(t, 2, 3)

    q, k, v, g, b = blocks(q), blocks(k), blocks(v), blocks(log_alpha), blocks(beta)
    gamma = jnp.cumsum(g, axis=-1)
    pos = jnp.arange(C)
    incl = pos[:, None] >= pos[None, :]
    strict = pos[:, None] > pos[None, :]
    diff = gamma[..., :, None] - gamma[..., None, :]
    decay = jnp.where(incl, jnp.exp(jnp.where(incl, diff, 0.0)), 0.0)
    kb = k * b[..., None]
    a_strict = jnp.where(strict, jnp.einsum('bnhid,bnhjd->bnhij', kb, k) * decay, 0.0)
    lhs = a_strict + jnp.eye(C, dtype=jnp.float32)
    rhs = jnp.concatenate([v * b[..., None], kb * jnp.exp(gamma)[..., None]], axis=-1)
    uw = lax.linalg.triangular_solve(lhs, rhs, left_side=True, lower=True, unit_diagonal=True)
    u, w = uw[..., :DV], uw[..., DV:]
    qk = jnp.einsum('bnhid,bnhjd->bnhij', q, k) * decay
    q_dec = q * jnp.exp(gamma)[..., None]
    k_dec = k * jnp.exp(gamma[..., -1:] - gamma)[..., None]
    chunk_decay = jnp.exp(gamma[..., -1])
    xs = tuple(jnp.moveaxis(t, 1, 0) for t in (u, w, qk, q_dec, k_dec, chunk_decay))

    def step(state, inp):
        u_c, w_c, qk_c, qd_c, kd_c, cd_c = inp
        v_new = u_c - jnp.einsum('bhck,bhkv->bhcv', w_c, state)
        o_c = jnp.einsum('bhck,bhkv->bhcv', qd_c, state) + jnp.einsum('bhij,bhjv->bhiv', qk_c, v_new)
        state = state * cd_c[..., None, None] + jnp.einsum('bhck,bhcv->bhkv', kd_c, v_new)
        return state, o_c

    state0 = jnp.zeros((B, H, DK, DV), jnp.float32)
    _, o = lax.scan(step, state0, xs)
    return jnp.transpose(o, (1, 0, 3, 2, 4)).reshape(B, S, H, DV)


def gated_deltanet_branch(qkv, alpha_logit, beta_logit, z, conv_w, a_log, dt_bias, o_norm):
    B, S, _ = qkv.shape
    f32 = jnp.float32
    qkv = jax.nn.silu(centred_depthwise_conv(qkv, conv_w)).astype(f32)
    q, k, v = jnp.split(qkv, 3, axis=-1)
    q = l2_normalize(q.reshape(B, S, H_A, DK_A)) * (DK_A ** -0.5)
    k = l2_normalize(k.reshape(B, S, H_A, DK_A))
    v = v.reshape(B, S, H_A, DV_A)
    dt = jax.nn.softplus(alpha_logit.astype(f32).reshape(B, S, 2, H_A) + dt_bias.astype(f32))
    log_alpha = -jnp.exp(a_log.astype(f32)) * dt
    beta = jax.nn.sigmoid(beta_logit.astype(f32).reshape(B, S, 2, H_A))
    o_fwd = chunked_delta_rule(q, k, v, log_alpha[:, :, 0], beta[:, :, 0])
    rev = lambda t: jnp.flip(t, axis=1)
    o_bwd = rev(chunked_delta_rule(rev(q), rev(k), rev(v), rev(log_alpha[:, :, 1]), rev(beta[:, :, 1])))
    o = rms_norm(o_fwd + o_bwd, o_norm) * jax.nn.silu(z.astype(f32).reshape(B, S, H_A, DV_A))
    return o.reshape(B, S, H_A * DV_A).astype(z.dtype)


def spatial_gating_branch(u, v, sg_norm, sg_w, sg_b):
    B, S, _ = u.shape
    n = S // SG_CHUNK
    u = jax.nn.gelu(u)
    v = rms_norm(jax.nn.gelu(v), sg_norm).reshape(B, n, SG_CHUNK, SG_GROUPS, SG_WIDTH // SG_GROUPS)
    mixed = jnp.einsum('gpq,bnqgc->bnpgc', sg_w, v) + sg_b.T[:, :, None]
    return u * mixed.reshape(B, S, SG_WIDTH)


def rope_tables(S):
    inv_freq = ROPE_THETA ** (-jnp.arange(0, ROPE_DIM, 2, dtype=jnp.float32) / ROPE_DIM)
    ang = jnp.arange(S, dtype=jnp.float32)[:, None] * inv_freq[None, :]
    return jnp.cos(ang), jnp.sin(ang)


def apply_rope(x, cos, sin):
    xf = x.astype(jnp.float32)
    x1, x2 = xf[..., :ROPE_DIM // 2], xf[..., ROPE_DIM // 2:]
    return jnp.concatenate([x1 * cos - x2 * sin, x2 * cos + x1 * sin], axis=-1).astype(x.dtype)


def mla_branch(c_q, c_kv, k_rope, q_a_norm, w_uq, kv_a_norm, w_ukv, q_nope_norm, q_rope_norm, k_nope_norm, k_rope_norm):
    B, S, _ = c_q.shape
    q = (rms_norm(c_q, q_a_norm) @ w_uq).reshape(B, S, H_C, QK_HEAD_DIM)
    kv = (rms_norm(c_kv, kv_a_norm) @ w_ukv).reshape(B, S, H_C, NOPE_DIM + DV_C)
    cos, sin = rope_tables(S)
    q_nope = rms_norm(q[..., :NOPE_DIM], q_nope_norm)
    q_rope = apply_rope(rms_norm(q[..., NOPE_DIM:], q_rope_norm), cos[:, None, :], sin[:, None, :])
    k_nope = rms_norm(kv[..., :NOPE_DIM], k_nope_norm)
    v = kv[..., NOPE_DIM:]
    k_rope = apply_rope(rms_norm(k_rope, k_rope_norm), cos, sin)
    q_all = jnp.concatenate([q_nope, q_rope], axis=-1)
    nb = S // Q_BLOCK
    q_blocks = jnp.moveaxis(q_all.reshape(B, nb, Q_BLOCK, H_C, QK_HEAD_DIM), 1, 0)

    def attend(qb):
        s = (jnp.einsum('bqhd,bkhd->bhqk', qb[..., :NOPE_DIM], k_nope)
             + jnp.einsum('bqhr,bkr->bhqk', qb[..., NOPE_DIM:], k_rope))
        p = jax.nn.softmax(s.astype(jnp.float32) * SM_SCALE, axis=-1)
        return jnp.einsum('bhqk,bkhv->bqhv', p.astype(v.dtype), v)

    o = lax.map(attend, q_blocks)
    return jnp.moveaxis(o, 0, 1).reshape(B, S, H_C * DV_C)


def peer_layer(x, wq, keys, u_tab, v_tab):
    B, S, D = x.shape
    x_blocks = x.reshape(-1, PEER_BLOCK, D)

    def retrieve(xt):
        q = (xt @ wq).reshape(PEER_BLOCK, PEER_HEADS, 2, PEER_QDIM // 2)
        s = jnp.einsum('thpd,hpkd->thpk', q, keys).astype(jnp.float32)
        top_s, top_i = lax.top_k(s, PEER_TOPK)
        cand_s = (top_s[:, :, 0, :, None] + top_s[:, :, 1, None, :]).reshape(PEER_BLOCK, PEER_HEADS, PEER_TOPK * PEER_TOPK)
        cand_i = (top_i[:, :, 0, :, None] * N_KEYS + top_i[:, :, 1, None, :]).reshape(PEER_BLOCK, PEER_HEADS, PEER_TOPK * PEER_TOPK)
        best_s, pos = lax.top_k(cand_s, PEER_TOPK)
        expert = jnp.take_along_axis(cand_i, pos, axis=-1)
        g = jax.nn.softmax(best_s, axis=-1)
        u_e = jnp.take(u_tab, expert, axis=0)
        h = jax.nn.gelu(jnp.einsum('thkd,td->thk', u_e, xt).astype(jnp.float32))
        v_e = jnp.take(v_tab, expert, axis=0)
        return jnp.einsum('thk,thkd->td', (g * h).astype(v_tab.dtype), v_e)

    return lax.map(retrieve, x_blocks).reshape(B, S, D)


def trunk_layer(x, mix_norm, w_in, conv_w, a_log, dt_bias, o_norm, sg_norm, sg_w, sg_b, q_a_norm, w_uq, kv_a_norm, w_ukv,
                q_nope_norm, q_rope_norm, k_nope_norm, k_rope_norm, w_branch, w_out, ffn_norm, peer_wq, peer_keys, peer_u, peer_v):
    B, S, _ = x.shape
    xn = rms_norm(x, mix_norm)
    proj = xn @ w_in
    split_points = np.cumsum(IN_SPLIT_SIZES)[:-1].tolist()
    qkv_a, alpha_a, beta_a, z_a, u_b, v_b, c_q, c_kv, k_rope, gate_logit = jnp.split(proj, split_points, axis=-1)
    out_a = gated_deltanet_branch(qkv_a, alpha_a, beta_a, z_a, conv_w, a_log, dt_bias, o_norm)
    out_b = spatial_gating_branch(u_b, v_b, sg_norm, sg_w, sg_b)
    out_c = mla_branch(c_q, c_kv, k_rope, q_a_norm, w_uq, kv_a_norm, w_ukv, q_nope_norm, q_rope_norm, k_nope_norm, k_rope_norm)
    gates = jax.nn.sigmoid(gate_logit.astype(jnp.float32)).reshape(B, S, N_BRANCH, D_MODEL)
    merged = (gates[:, :, 0] * (out_a @ w_branch[0]).astype(jnp.float32)
              + gates[:, :, 1] * (out_b @ w_branch[1]).astype(jnp.float32)
              + gates[:, :, 2] * (out_c @ w_branch[2]).astype(jnp.float32))
    h = x + merged.astype(x.dtype) @ w_out
    return h + peer_layer(rms_norm(h, ffn_norm), peer_wq, peer_keys, peer_u, peer_v)


def setup_inputs(seed: int = 0) -> dict:
    key = jax.random.key(seed)
    ks = jax.random.split(key, 32)
    L = DEPTH
    f32 = jnp.float32

    def nrm(k, shape, scale):
        return jax.random.normal(k, shape, f32) * scale

    def gain(k, n):
        return 1.0 + 0.02 * jax.random.normal(k, (L, n), f32)

    dt = jnp.exp(jax.random.uniform(ks[5], (L, 2, H_A), f32, math.log(1e-3), math.log(1e-1)))
    return {
        'x_prompt': nrm(ks[0], (BATCH, SEQ, D_MODEL), 1.0),
        'x_sample': nrm(ks[1], (DEC_BATCH, DEC_SEQ, D_MODEL), 1.0),
        'mix_norm': gain(ks[2], D_MODEL),
        'w_in': nrm(ks[3], (L, D_MODEL, IN_COLS), D_MODEL ** -0.5),
        'conv_w': nrm(ks[4], (L, CONV_W, QKV_A_COLS), CONV_W ** -0.5),
        'a_log': jnp.log(jax.random.uniform(ks[6], (L, 2, H_A), f32, 1.0, 16.0)),
        'dt_bias': dt + jnp.log(-jnp.expm1(-dt)),
        'o_norm': gain(ks[7], DV_A),
        'sg_norm': gain(ks[8], SG_WIDTH),
        'sg_w': nrm(ks[9], (L, SG_GROUPS, SG_CHUNK, SG_CHUNK), SG_CHUNK ** -0.5),
        'sg_b': nrm(ks[10], (L, SG_GROUPS, SG_CHUNK), 0.02),
        'q_a_norm': gain(ks[11], Q_LORA),
        'w_uq': nrm(ks[12], (L, Q_LORA, H_C * QK_HEAD_DIM), Q_LORA ** -0.5),
        'kv_a_norm': gain(ks[13], KV_LORA),
        'w_ukv': nrm(ks[14], (L, KV_LORA, H_C * (NOPE_DIM + DV_C)), KV_LORA ** -0.5),
        'q_nope_norm': gain(ks[15], NOPE_DIM),
        'q_rope_norm': gain(ks[16], ROPE_DIM),
        'k_nope_norm': gain(ks[17], NOPE_DIM),
        'k_rope_norm': gain(ks[18], ROPE_DIM),
        'w_branch': nrm(ks[19], (L, N_BRANCH, BRANCH_WIDTH, D_MODEL), BRANCH_WIDTH ** -0.5),
        'w_out': nrm(ks[20], (L, D_MODEL, D_MODEL), D_MODEL ** -0.5),
        'ffn_norm': gain(ks[21], D_MODEL),
        'peer_wq': nrm(ks[22], (L, D_MODEL, PEER_HEADS * PEER_QDIM), D_MODEL ** -0.5),
        'peer_keys': nrm(ks[23], (L, PEER_HEADS, 2, N_KEYS, PEER_QDIM // 2), (PEER_QDIM // 2) ** -0.5),
        'peer_u': nrm(ks[24], (L, N_EXPERTS, D_MODEL), D_MODEL ** -0.5),
        'peer_v': nrm(ks[25], (L, N_EXPERTS, D_MODEL), (PEER_HEADS * PEER_TOPK) ** -0.5),
    }


def reference(x_prompt, x_sample, mix_norm, w_in, conv_w, a_log, dt_bias, o_norm, sg_norm, sg_w, sg_b, q_a_norm, w_uq,
              kv_a_norm, w_ukv, q_nope_norm, q_rope_norm, k_nope_norm, k_rope_norm, w_branch, w_out, ffn_norm,
              peer_wq, peer_keys, peer_u, peer_v):
    def run_trunk(x):
        for l in range(DEPTH):
            x = trunk_layer(x, mix_norm[l], w_in[l], conv_w[l], a_log[l], dt_bias[l], o_norm[l], sg_norm[l], sg_w[l], sg_b[l],
                            q_a_norm[l], w_uq[l], kv_a_norm[l], w_ukv[l], q_nope_norm[l], q_rope_norm[l], k_nope_norm[l],
                            k_rope_norm[l], w_branch[l], w_out[l], ffn_norm[l], peer_wq[l], peer_keys[l], peer_u[l], peer_v[l])
        return x

    y_prompt = run_trunk(x_prompt)
    y_sample = run_trunk(x_sample)
    return (y_prompt, y_sample)
```

```python
import numpy as np
from contextlib import ExitStack
import concourse.bass as bass
import concourse.mybir as mybir
from concourse.bass_utils import run_bass_kernel_spmd

F32 = mybir.dt.float32
BF16 = mybir.dt.bfloat16
I32 = mybir.dt.int32
U32 = mybir.dt.uint32
AF = mybir.ActivationFunctionType
ALU = mybir.AluOpType
AX = mybir.AxisListType

D = 1024
IN_COLS = 6864
EPS = 1e-6
C_AB, C_Z, C_U, C_V, C_CQ, C_CKV, C_KR, C_G = 1536, 1552, 2064, 2576, 3088, 3472, 3728, 3792
SM_SCALE = 192 ** -0.5
SEM_EPOCH = 1 << 28


class Buf:
    def __init__(self, name, ap, space):
        self.name = name
        self.ap = ap
        self.space = space
        self.lw = {}
        self.rd = {}
        self.sem = None
        self.semcnt = 0

    def __getitem__(self, key):
        return self.ap[key]

    def rearrange(self, *a, **kw):
        return self.ap.rearrange(*a, **kw)


class Eng:
    def __init__(self, name, obj):
        self.name = name
        self.obj = obj
        self.sem = None
        self.cnt = 0
        self.waited = {}


class KB:
    def __init__(self, nc, es):
        self.nc = nc
        self.es = es
        self.eng = {}
        self.nsem = 0
        self.n_ops = 0
        self.n_waits = 0
        self.sem_pool = []
        self.ps = None
        self.pass_bufs = []

    def new_sem(self, name):
        self.nsem += 1
        return self.es.enter_context(self.nc.semaphore(name + "_%d" % self.nsem))

    def setup_engines(self):
        nc = self.nc
        for name, obj in dict(pe=nc.tensor, act=nc.scalar, dve=nc.vector, pool=nc.gpsimd, sp=nc.sync).items():
            e = Eng(name, obj)
            e.sem = self.new_sem("prog_" + name)
            self.eng[name] = e

    def begin_pass(self, ps):
        self.ps = ps
        self.pass_bufs = []
        self.uid = getattr(self, "uid", 0) + 1

    def end_pass(self):
        deps = {}
        for e in self.eng.values():
            if e.cnt > 0:
                deps[id(e.sem)] = (e.sem, e.cnt)
        for b in self.pass_bufs:
            if b.sem is not None and b.semcnt > 0:
                deps[id(b.sem)] = (b.sem, b.semcnt)
        for e in self.eng.values():
            self._emit_waits(e, deps)
        for b in self.pass_bufs:
            if b.sem is not None:
                self.sem_pool.append((b.sem, b.semcnt))
                b.sem = None
        self.pass_bufs = []

    def sbuf(self, name, shape, dtype):
        t = self.ps.enter_context(self.nc.sbuf_tensor("%s_u%d" % (name, self.uid), list(shape), dtype))
        b = Buf(name, t.ap(), "sbuf")
        self.pass_bufs.append(b)
        return b

    def psum(self, name, shape, dtype):
        t = self.ps.enter_context(self.nc.psum_tensor("%s_u%d" % (name, self.uid), list(shape), dtype))
        b = Buf(name, t.ap(), "psum")
        self.pass_bufs.append(b)
        return b

    def dram(self, name, shape, dtype, kind="Internal"):
        t = self.nc.dram_tensor(name, list(shape), dtype, kind=kind)
        return Buf(name, t.ap(), "dram")

    def _collect(self, reads, writes):
        deps = {}

        def add(d):
            for sid, (sem, val) in d.items():
                if sid not in deps or deps[sid][1] < val:
                    deps[sid] = (sem, val)
        for b in reads:
            add(b.lw)
        for b in writes:
            add(b.lw)
            add(b.rd)
        return deps

    def _emit_waits(self, e, deps, skip_own=False, keep_one=False):
        need = []
        for sid, (sem, val) in deps.items():
            if skip_own and sem is e.sem:
                continue
            if e.waited.get(sid, 0) >= val:
                continue
            need.append((sid, sem, val))
        kept = None
        if keep_one and need:
            kept = need.pop()
        for sid, sem, val in need:
            e.obj.wait_ge(sem, val)
            self.n_waits += 1
            e.waited[sid] = val
        if kept is not None:
            e.waited[kept[0]] = kept[2]
            return (kept[1], kept[2])
        return None

    def op(self, en, fn, r=(), w=(), embed=True):
        e = self.eng[en]
        if e.cnt >= SEM_EPOCH:
            e.sem = self.new_sem("prog_" + en)
            e.cnt = 0
        w = list(w) + [b for b in r if b.space == "psum" and b not in w]
        deps = self._collect(r, w)
        kept = self._emit_waits(e, deps, skip_own=(en == "pe"), keep_one=embed)
        ins = fn(e.obj)
        if kept is not None:
            ins._wait_ge(kept[0], kept[1])
        e.cnt += 1
        ins.then_inc(e.sem, 1)
        self.n_ops += 1
        mark = (e.sem, e.cnt)
        sid = id(e.sem)
        for b in w:
            b.lw = {sid: mark}
            b.rd = {}
        for b in r:
            if b not in w:
                b.rd[sid] = mark
        return ins

    def dma(self, qn, out_buf, out_ap, in_buf, in_ap, idx=None, **kw):
        e = self.eng[qn]
        sb = out_buf if out_buf.space == "sbuf" else in_buf
        assert sb.space == "sbuf"
        if sb.sem is None:
            if self.sem_pool:
                sb.sem, sb.semcnt = self.sem_pool.pop()
            else:
                sb.sem = self.new_sem("d_" + sb.name)
                sb.semcnt = 0
        extra_r = [idx[0]] if idx is not None else []
        deps = self._collect([in_buf] + extra_r, [out_buf])
        kept = self._emit_waits(e, deps, keep_one=True)
        if idx is not None:
            ins = e.obj.indirect_dma_start(out=out_ap, out_offset=None, in_=in_ap,
                                           in_offset=bass.IndirectOffsetOnAxis(idx[1], 0), **kw)
        else:
            ins = e.obj.dma_start(out=out_ap, in_=in_ap, **kw)
        if kept is not None:
            ins._wait_ge(kept[0], kept[1])
        sb.semcnt += 16
        ins.then_inc(sb.sem, 16)
        self.n_ops += 1
        mark = (sb.sem, sb.semcnt)
        sid = id(sb.sem)
        if out_buf.space == "dram":
            out_buf.lw[sid] = mark
            out_buf.rd = {}
        else:
            out_buf.lw = {sid: mark}
            out_buf.rd = {}
        in_buf.rd[sid] = mark
        for b in extra_r:
            b.rd[sid] = mark
        return ins

    def final_wait(self, en, bufs):
        e = self.eng[en]
        deps = {}
        for b in bufs:
            for sid, sv in b.lw.items():
                deps[sid] = sv
        self._emit_waits(e, deps)


class Rot:
    def __init__(self, bufs):
        self.bufs = bufs
        self.i = 0

    def next(self):
        b = self.bufs[self.i % len(self.bufs)]
        self.i += 1
        return b


def gelu_ops(k, src, xs, tmp, out, n, src_is_psum=True):
    k.op("act", lambda e: e.activation(out=xs[:, 0:n], in_=src[:, 0:n], func=AF.Copy), r=[src], w=[xs])
    k.op("dve", lambda e: e.tensor_tensor(out=tmp[:, 0:n], in0=xs[:, 0:n], in1=xs[:, 0:n], op=ALU.mult), r=[xs], w=[tmp])
    k.op("dve", lambda e: e.tensor_scalar(out=tmp[:, 0:n], in0=tmp[:, 0:n], scalar1=0.044715, scalar2=1.0, op0=ALU.mult, op1=ALU.add), r=[tmp], w=[tmp])
    k.op("dve", lambda e: e.tensor_tensor(out=tmp[:, 0:n], in0=tmp[:, 0:n], in1=xs[:, 0:n], op=ALU.mult), r=[tmp, xs], w=[tmp])
    k.op("act", lambda e: e.activation(out=tmp[:, 0:n], in_=tmp[:, 0:n], func=AF.Sigmoid, scale=1.5957691216057308), r=[tmp], w=[tmp])
    k.op("dve", lambda e: e.tensor_tensor(out=out[:, 0:n], in0=tmp[:, 0:n], in1=xs[:, 0:n], op=ALU.mult), r=[tmp, xs], w=[out])


def rstd_ops(k, ss, out, n_cols, inv_n, eps=EPS):
    k.op("dve", lambda e: e.tensor_scalar(out=out[:, 0:n_cols], in0=ss[:, 0:n_cols], scalar1=inv_n, scalar2=eps, op0=ALU.mult, op1=ALU.add), r=[ss], w=[out])
    k.op("act", lambda e: e.activation(out=out[:, 0:n_cols], in_=out[:, 0:n_cols], func=AF.Sqrt), r=[out], w=[out])
    k.op("dve", lambda e: e.reciprocal(out=out[:, 0:n_cols], in_=out[:, 0:n_cols]), r=[out], w=[out])


def sumsq_ops(k, src, junk, ss_ap, ss_buf, n0, n1):
    k.op("act", lambda e: e.activation(out=junk[:, 0:n1 - n0], in_=src[:, n0:n1], func=AF.Square, accum_out=ss_ap), r=[src], w=[junk, ss_buf])


def load_cast(k, q, dst, dst_ap, src_buf, src_ap, stage_rot, shape2):
    st = stage_rot.next()
    p, n = shape2
    k.dma(q, st, st[0:p, 0:n], src_buf, src_ap)
    k.op("pool", lambda e: e.tensor_copy(out=dst_ap, in_=st[0:p, 0:n]), r=[st], w=[dst])


def pass_a(k, cfg, T, l, xin):
    NT = cfg["NT"]
    ntile = NT // 128
    W = T["W"]
    with ExitStack() as ps:
        k.begin_pass(ps)
        w_in = k.sbuf("w_in", [128, 8, IN_COLS], BF16)
        w_uq = k.sbuf("w_uq", [128, 3, 768], BF16)
        w_ukv = k.sbuf("w_ukv", [128, 2, 1024], BF16)
        sgwT = k.sbuf("sgwT", [128, 4, 128], BF16)
        stage = Rot([k.sbuf("stg%d" % i, [128, 1144], F32) for i in range(2)])
        for kc in range(8):
            for j in range(6):
                load_cast(k, "sp", w_in, w_in[:, kc, j * 1144:(j + 1) * 1144], W["w_in"], W["w_in"][l, kc * 128:(kc + 1) * 128, j * 1144:(j + 1) * 1144], stage, (128, 1144))
        for kc in range(3):
            load_cast(k, "sp", w_uq, w_uq[:, kc, :], W["w_uq"], W["w_uq"][l, kc * 128:(kc + 1) * 128, :], stage, (128, 768))
        for kc in range(2):
            load_cast(k, "sp", w_ukv, w_ukv[:, kc, :], W["w_ukv"], W["w_ukv"][l, kc * 128:(kc + 1) * 128, :], stage, (128, 1024))
        load_cast(k, "sp", sgwT, sgwT[:, :, :].rearrange("p a b -> p (a b)"), W["sg_wT"], W["sg_wT"][l].rearrange("p a b -> p (a b)"), stage, (128, 512))
        def ld(name, n):
            b = k.sbuf(name, [128, n], F32)
            k.dma("sp", b, b[:, :], W[name], W[name][l])
            return b
        mixg = ld("mixg", 1024)
        sgn = ld("sgn", 512)
        qag = ld("qag", 384)
        kvag = ld("kvag", 256)
        qng = ld("qng", 128)
        qrg = ld("qrg", 64)
        kng = ld("kng", 128)
        krg = ld("krg", 64)
        sgb = ld("sg_bT", 4)
        ident = k.sbuf("ident_bf", [128, 128], BF16)
        identf = k.sbuf("ident_f", [128, 128], F32)
        k.dma("sp", identf, identf[:, :], W["ident"], W["ident"][:, :])
        k.op("dve", lambda e: e.tensor_copy(out=ident[:, :], in_=identf[:, :]), r=[identf], w=[ident])

        xt_r = Rot([k.sbuf("xt%d" % i, [128, 1024], F32) for i in range(1)])
        junk = k.sbuf("junk", [128, 1024], BF16)
        st1 = k.sbuf("st1", [128, 16], F32)
        st2 = k.sbuf("st2", [128, 16], F32)
        xn = k.sbuf("xn", [128, 1024], BF16)
        xnT = k.sbuf("xnT", [128, 8, 128], BF16)
        psT = k.psum("psT", [128, 8, 128], BF16)
        psA = Rot([k.psum("psA%d" % i, [128, 512], F32) for i in range(3)])
        psB = k.psum("psB", [128, 512], F32)
        qkv_st = Rot([k.sbuf("qkv_st%d" % i, [128, 12, 128], F32) for i in range(1)])
        ab_st = Rot([k.sbuf("ab_st%d" % i, [16, 128], F32) for i in range(2)])
        zs_st = Rot([k.sbuf("zs_st%d" % i, [128, 512], F32) for i in range(1)])
        ob_st = Rot([k.sbuf("ob_st%d" % i, [128, 512], BF16) for i in range(2)])
        g_st = Rot([k.sbuf("g_st%d" % i, [128, 1024], BF16) for i in range(2)])
        xs = k.sbuf("xs", [128, 512], F32)
        tmp = k.sbuf("tmp", [128, 512], F32)
        gu = k.sbuf("gu", [128, 512], F32)
        gv = k.sbuf("gv", [128, 512], F32)
        vn = k.sbuf("vn", [128, 512], BF16)
        c_sb = k.sbuf("c_sb", [128, 704], F32)
        cn = k.sbuf("cn", [128, 640], BF16)
        cnT = k.sbuf("cnT", [128, 5, 128], BF16)
        q_sb = k.sbuf("q_sb", [128, 4, 192], F32)
        kv_sb = k.sbuf("kv_sb", [128, 4, 256], F32)
        sq = k.sbuf("sq", [128, 1024], F32)
        qa = k.sbuf("qa", [128, 4, 192], BF16)
        kn = k.sbuf("kn", [128, 4, 128], BF16)
        v_st = Rot([k.sbuf("v_st%d" % i, [128, 4, 128], BF16) for i in range(2)])
        kr = k.sbuf("kr", [128, 64], F32)
        krb = k.sbuf("krb", [128, 64], BF16)
        cs_r = Rot([k.sbuf("cs%d" % i, [128, 64], F32) for i in range(2)])
        rt = k.sbuf("rt", [128, 4, 64], F32)
        qT_st = Rot([k.sbuf("qT_st%d" % i, [128, 4, 2, 128], BF16) for i in range(1)])
        kT_st = Rot([k.sbuf("kT_st%d" % i, [128, 5, 128], BF16) for i in range(1)])

        tile_pos = cfg["tile_pos"]
        for t in range(ntile):
            t0 = t * 128
            xt = xt_r.next()
            k.dma("sp", xt, xt[:, :], xin, xin[t0:t0 + 128, :])
            k.op("act", lambda e: e.activation(out=junk[:, :], in_=xt[:, :], func=AF.Square, accum_out=st1[:, 0:1]), r=[xt], w=[junk, st1])
            rstd_ops(k, st1, st2, 1, 1.0 / 1024)
            k.op("dve", lambda e: e.scalar_tensor_tensor(out=xn[:, :], in0=xt[:, :], scalar=st2[:, 0:1], in1=mixg[:, :], op0=ALU.mult, op1=ALU.mult), r=[xt, st2, mixg], w=[xn])
            for kc in range(8):
                k.op("pe", lambda e: e.transpose(out=psT[:, kc, :], in_=xn[:, kc * 128:(kc + 1) * 128], identity=ident[:, :]), r=[xn, ident], w=[psT])
            k.op("act", lambda e: e.activation(out=xnT[:, :, :], in_=psT[:, :, :], func=AF.Copy), r=[psT], w=[xnT])
            qs = qkv_st.next()
            for c4 in range(3):
                pa = psA.next()
                for cc in range(4):
                    c = c4 * 4 + cc
                    for kc in range(8):
                        k.op("pe", lambda e: e.matmul(pa[:, cc * 128:(cc + 1) * 128], lhsT=w_in[:, kc, c * 128:(c + 1) * 128], rhs=xnT[:, kc, :], start=(kc == 0), stop=(kc == 7)), r=[w_in, xnT], w=[pa])
                k.op("act" if c4 % 2 == 0 else "dve",
                     (lambda e: e.activation(out=qs[:, c4 * 4:(c4 + 1) * 4, :].rearrange("p a b -> p (a b)"), in_=pa[:, :], func=AF.Copy)) if c4 % 2 == 0 else
                     (lambda e: e.tensor_copy(out=qs[:, c4 * 4:(c4 + 1) * 4, :].rearrange("p a b -> p (a b)"), in_=pa[:, :])), r=[pa], w=[qs])
            k.dma("pool", T["qkvT"], T["qkvT"][:, t0:t0 + 128].rearrange("(c p) t -> p c t", p=128), qs, qs[:, :, :])
            pa = psA.next()
            for kc in range(8):
                k.op("pe", lambda e: e.matmul(pa[0:16, 0:128], lhsT=w_in[:, kc, C_AB:C_AB + 16], rhs=xnT[:, kc, :], start=(kc == 0), stop=(kc == 7)), r=[w_in, xnT], w=[pa])
            ab = ab_st.next()
            k.op("dve", lambda e: e.tensor_copy(out=ab[:, :], in_=pa[0:16, 0:128]), r=[pa], w=[ab])
            k.dma("pool", T["abT"], T["abT"][:, t0:t0 + 128], ab, ab[:, :])

            def tok_mm(pa, c0, n):
                for kc in range(8):
                    k.op("pe", lambda e: e.matmul(pa[:, 0:n], lhsT=xnT[:, kc, :], rhs=w_in[:, kc, c0:c0 + n], start=(kc == 0), stop=(kc == 7)), r=[w_in, xnT], w=[pa])
            pa = psA.next()
            tok_mm(pa, C_Z, 512)
            zs = zs_st.next()
            k.op("act", lambda e: e.activation(out=zs[:, :], in_=pa[:, :], func=AF.Silu), r=[pa], w=[zs])
            k.dma("pool", T["zs"], T["zs"][t0:t0 + 128, :], zs, zs[:, :])
            pa = psA.next()
            tok_mm(pa, C_U, 512)
            gelu_ops(k, pa, xs, tmp, gu, 512)
            pa = psA.next()
            tok_mm(pa, C_V, 512)
            gelu_ops(k, pa, xs, tmp, gv, 512)
            k.op("act", lambda e: e.activation(out=junk[:, 0:512], in_=gv[:, :], func=AF.Square, accum_out=st1[:, 1:2]), r=[gv], w=[junk, st1])
            rstd_ops(k, st1, st2, 2, 1.0 / 512)
            k.op("dve", lambda e: e.scalar_tensor_tensor(out=vn[:, :], in0=gv[:, :], scalar=st2[:, 1:2], in1=sgn[:, :], op0=ALU.mult, op1=ALU.mult), r=[gv, st2, sgn], w=[vn])
            for g in range(4):
                k.op("pe", lambda e: e.matmul(psB[:, g * 128:(g + 1) * 128], lhsT=sgwT[:, g, :], rhs=vn[:, g * 128:(g + 1) * 128], start=True, stop=True), r=[sgwT, vn], w=[psB])
            ob = ob_st.next()
            for g in range(4):
                k.op("dve", lambda e: e.scalar_tensor_tensor(out=ob[:, g * 128:(g + 1) * 128], in0=psB[:, g * 128:(g + 1) * 128], scalar=sgb[:, g:g + 1], in1=gu[:, g * 128:(g + 1) * 128], op0=ALU.add, op1=ALU.mult), r=[psB, sgb, gu], w=[ob])
            k.dma("pool", T["ob"], T["ob"][t0:t0 + 128, :], ob, ob[:, :])
            pa = psA.next()
            tok_mm(pa, C_CQ, 512)
            k.op("act", lambda e: e.activation(out=c_sb[:, 0:512], in_=pa[:, :], func=AF.Copy), r=[pa], w=[c_sb])
            pa = psA.next()
            tok_mm(pa, C_CQ + 512, 192)
            k.op("act", lambda e: e.activation(out=c_sb[:, 512:704], in_=pa[:, 0:192], func=AF.Copy), r=[pa], w=[c_sb])
            k.op("act", lambda e: e.activation(out=junk[:, 0:384], in_=c_sb[:, 0:384], func=AF.Square, accum_out=st1[:, 2:3]), r=[c_sb], w=[junk, st1])
            k.op("act", lambda e: e.activation(out=junk[:, 0:256], in_=c_sb[:, 384:640], func=AF.Square, accum_out=st1[:, 3:4]), r=[c_sb], w=[junk, st1])
            k.op("act", lambda e: e.activation(out=junk[:, 0:64], in_=c_sb[:, 640:704], func=AF.Square, accum_out=st1[:, 4:5]), r=[c_sb], w=[junk, st1])
            k.op("dve", lambda e: e.tensor_scalar(out=st1[:, 2:3], in0=st1[:, 2:3], scalar1=1.0 / 384, scalar2=None, op0=ALU.mult), r=[st1], w=[st1])
            k.op("dve", lambda e: e.tensor_scalar(out=st1[:, 3:4], in0=st1[:, 3:4], scalar1=1.0 / 256, scalar2=None, op0=ALU.mult), r=[st1], w=[st1])
            k.op("dve", lambda e: e.tensor_scalar(out=st1[:, 4:5], in0=st1[:, 4:5], scalar1=1.0 / 64, scalar2=None, op0=ALU.mult), r=[st1], w=[st1])
            k.op("dve", lambda e: e.tensor_scalar(out=st2[:, 2:5], in0=st1[:, 2:5], scalar1=EPS, scalar2=None, op0=ALU.add), r=[st1], w=[st2])
            k.op("act", lambda e: e.activation(out=st2[:, 2:5], in_=st2[:, 2:5], func=AF.Sqrt), r=[st2], w=[st2])
            k.op("dve", lambda e: e.reciprocal(out=st2[:, 2:5], in_=st2[:, 2:5]), r=[st2], w=[st2])
            k.op("dve", lambda e: e.scalar_tensor_tensor(out=cn[:, 0:384], in0=c_sb[:, 0:384], scalar=st2[:, 2:3], in1=qag[:, :], op0=ALU.mult, op1=ALU.mult), r=[c_sb, st2, qag], w=[cn])
            k.op("dve", lambda e: e.scalar_tensor_tensor(out=cn[:, 384:640], in0=c_sb[:, 384:640], scalar=st2[:, 3:4], in1=kvag[:, :], op0=ALU.mult, op1=ALU.mult), r=[c_sb, st2, kvag], w=[cn])
            k.op("dve", lambda e: e.scalar_tensor_tensor(out=kr[:, :], in0=c_sb[:, 640:704], scalar=st2[:, 4:5], in1=krg[:, :], op0=ALU.mult, op1=ALU.mult), r=[c_sb, st2, krg], w=[kr])
            for j in range(5):
                k.op("pe", lambda e: e.transpose(out=psT[:, j, :], in_=cn[:, j * 128:(j + 1) * 128], identity=ident[:, :]), r=[cn, ident], w=[psT])
            k.op("act", lambda e: e.activation(out=cnT[:, :, :], in_=psT[:, 0:5, :], func=AF.Copy), r=[psT], w=[cnT])
            pa = psA.next()
            for kc in range(3):
                k.op("pe", lambda e: e.matmul(pa[:, 0:512], lhsT=cnT[:, kc, :], rhs=w_uq[:, kc, 0:512], start=(kc == 0), stop=(kc == 2)), r=[cnT, w_uq], w=[pa])
            k.op("act", lambda e: e.activation(out=q_sb[:, :, :].rearrange("p a b -> p (a b)")[:, 0:512], in_=pa[:, 0:512], func=AF.Copy), r=[pa], w=[q_sb])
            pa = psA.next()
            for kc in range(3):
                k.op("pe", lambda e: e.matmul(pa[:, 0:256], lhsT=cnT[:, kc, :], rhs=w_uq[:, kc, 512:768], start=(kc == 0), stop=(kc == 2)), r=[cnT, w_uq], w=[pa])
            k.op("act", lambda e: e.activation(out=q_sb[:, :, :].rearrange("p a b -> p (a b)")[:, 512:768], in_=pa[:, 0:256], func=AF.Copy), r=[pa], w=[q_sb])
            for hf in range(2):
                pa = psA.next()
                for kc in range(2):
                    k.op("pe", lambda e: e.matmul(pa[:, 0:512], lhsT=cnT[:, 3 + kc, :], rhs=w_ukv[:, kc, hf * 512:(hf + 1) * 512], start=(kc == 0), stop=(kc == 1)), r=[cnT, w_ukv], w=[pa])
                k.op("act", lambda e: e.activation(out=kv_sb[:, :, :].rearrange("p a b -> p (a b)")[:, hf * 512:(hf + 1) * 512], in_=pa[:, 0:512], func=AF.Copy), r=[pa], w=[kv_sb])
            sq3 = sq[:, 0:768].rearrange("p (a b) -> p a b", a=4)
            k.op("dve", lambda e: e.tensor_tensor(out=sq3, in0=q_sb[:, :, :], in1=q_sb[:, :, :], op=ALU.mult), r=[q_sb], w=[sq])
            k.op("dve", lambda e: e.tensor_reduce(out=st1[:, 5:9], in_=sq3[:, :, 0:128], axis=AX.X, op=ALU.add), r=[sq], w=[st1])
            k.op("dve", lambda e: e.tensor_reduce(out=st1[:, 9:13], in_=sq3[:, :, 128:192], axis=AX.X, op=ALU.add), r=[sq], w=[st1])
            k.op("dve", lambda e: e.tensor_scalar(out=st2[:, 5:9], in0=st1[:, 5:9], scalar1=1.0 / 128, scalar2=EPS, op0=ALU.mult, op1=ALU.add), r=[st1], w=[st2])
            k.op("dve", lambda e: e.tensor_scalar(out=st2[:, 9:13], in0=st1[:, 9:13], scalar1=1.0 / 64, scalar2=EPS, op0=ALU.mult, op1=ALU.add), r=[st1], w=[st2])
            sk3 = sq[:, 0:512].rearrange("p (a b) -> p a b", a=4)
            k.op("dve", lambda e: e.tensor_tensor(out=sk3, in0=kv_sb[:, :, 0:128], in1=kv_sb[:, :, 0:128], op=ALU.mult), r=[kv_sb, st1], w=[sq])
            k.op("dve", lambda e: e.tensor_reduce(out=st1[:, 13:16], in_=sk3[:, 0:3, :], axis=AX.X, op=ALU.add), r=[sq], w=[st1])
            k.op("dve", lambda e: e.tensor_reduce(out=st1[:, 0:1], in_=sk3[:, 3:4, :], axis=AX.X, op=ALU.add), r=[sq], w=[st1])
            k.op("dve", lambda e: e.tensor_scalar(out=st2[:, 13:16], in0=st1[:, 13:16], scalar1=1.0 / 128, scalar2=EPS, op0=ALU.mult, op1=ALU.add), r=[st1], w=[st2])
            k.op("dve", lambda e: e.tensor_scalar(out=st2[:, 0:1], in0=st1[:, 0:1], scalar1=1.0 / 128, scalar2=EPS, op0=ALU.mult, op1=ALU.add), r=[st1], w=[st2])
            k.op("act", lambda e: e.activation(out=st2[:, 0:16], in_=st2[:, 0:16], func=AF.Sqrt), r=[st2], w=[st2])
            k.op("dve", lambda e: e.reciprocal(out=st2[:, 0:16], in_=st2[:, 0:16]), r=[st2], w=[st2])
            k.op("dve", lambda e: e.tensor_tensor(out=sq3[:, :, 0:128], in0=q_sb[:, :, 0:128], in1=st2[:, 5:9].unsqueeze(2).to_broadcast([128, 4, 128]), op=ALU.mult), r=[q_sb, st2], w=[sq])
            k.op("dve", lambda e: e.tensor_tensor(out=qa[:, :, 0:128], in0=sq3[:, :, 0:128], in1=qng[:, :].unsqueeze(1).to_broadcast([128, 4, 128]), op=ALU.mult), r=[sq, qng], w=[qa])
            cs = cs_r.next()
            p0 = tile_pos[t]
            k.dma("sp", cs, cs[:, :], W["rope"], W["rope"][p0:p0 + 128, :])
            k.op("dve", lambda e: e.tensor_tensor(out=rt[:, :, :], in0=q_sb[:, :, 128:192], in1=st2[:, 9:13].unsqueeze(2).to_broadcast([128, 4, 64]), op=ALU.mult), r=[q_sb, st2], w=[rt])
            k.op("dve", lambda e: e.tensor_tensor(out=rt[:, :, :], in0=rt[:, :, :], in1=qrg[:, :].unsqueeze(1).to_broadcast([128, 4, 64]), op=ALU.mult), r=[rt, qrg], w=[rt])
            cosb = cs[:, 0:32].unsqueeze(1).to_broadcast([128, 4, 32])
            sinb = cs[:, 32:64].unsqueeze(1).to_broadcast([128, 4, 32])
            t3 = sq[:, 0:256].rearrange("p (a b) -> p a b", a=4)
            k.op("dve", lambda e: e.tensor_tensor(out=t3[:, :, 0:32], in0=rt[:, :, 0:32], in1=cosb, op=ALU.mult), r=[rt, cs], w=[sq])
            k.op("dve", lambda e: e.tensor_tensor(out=t3[:, :, 32:64], in0=rt[:, :, 32:64], in1=sinb, op=ALU.mult), r=[rt, cs], w=[sq])
            k.op("dve", lambda e: e.tensor_tensor(out=qa[:, :, 128:160], in0=t3[:, :, 0:32], in1=t3[:, :, 32:64], op=ALU.subtract), r=[sq], w=[qa])
            k.op("dve", lambda e: e.tensor_tensor(out=t3[:, :, 0:32], in0=rt[:, :, 32:64], in1=cosb, op=ALU.mult), r=[rt, cs], w=[sq])
            k.op("dve", lambda e: e.tensor_tensor(out=t3[:, :, 32:64], in0=rt[:, :, 0:32], in1=sinb, op=ALU.mult), r=[rt, cs], w=[sq])
            k.op("dve", lambda e: e.tensor_tensor(out=qa[:, :, 160:192], in0=t3[:, :, 0:32], in1=t3[:, :, 32:64], op=ALU.add), r=[sq], w=[qa])
            for h in range(4):
                col = 13 + h if h < 3 else 0
                k.op("dve", lambda e: e.scalar_tensor_tensor(out=kn[:, h, :], in0=kv_sb[:, h, 0:128], scalar=st2[:, col:col + 1], in1=kng[:, :], op0=ALU.mult, op1=ALU.mult), r=[kv_sb, st2, kng], w=[kn])
            k.op("dve", lambda e: e.tensor_tensor(out=sq[:, 0:32], in0=kr[:, 0:32], in1=cs[:, 0:32], op=ALU.mult), r=[kr, cs], w=[sq])
            k.op("dve", lambda e: e.tensor_tensor(out=sq[:, 32:64], in0=kr[:, 32:64], in1=cs[:, 32:64], op=ALU.mult), r=[kr, cs], w=[sq])
            k.op("dve", lambda e: e.tensor_tensor(out=krb[:, 0:32], in0=sq[:, 0:32], in1=sq[:, 32:64], op=ALU.subtract), r=[sq], w=[krb])
            k.op("dve", lambda e: e.tensor_tensor(out=sq[:, 0:32], in0=kr[:, 32:64], in1=cs[:, 0:32], op=ALU.mult), r=[kr, cs], w=[sq])
            k.op("dve", lambda e: e.tensor_tensor(out=sq[:, 32:64], in0=kr[:, 0:32], in1=cs[:, 32:64], op=ALU.mult), r=[kr, cs], w=[sq])
            k.op("dve", lambda e: e.tensor_tensor(out=krb[:, 32:64], in0=sq[:, 0:32], in1=sq[:, 32:64], op=ALU.add), r=[sq], w=[krb])
            vs = v_st.next()
            k.op("pool", lambda e: e.tensor_copy(out=vs[:, :, :], in_=kv_sb[:, :, 128:256]), r=[kv_sb], w=[vs])
            k.dma("pool", T["V"], T["V"][t0:t0 + 128, :].rearrange("p (a b) -> p a b", a=4), vs, vs[:, :, :])
            psQ = psT[:, :, :].rearrange("p (a b) c -> p a b c", a=4)
            for h in range(4):
                k.op("pe", lambda e: e.transpose(out=psQ[:, h, 0, :], in_=qa[:, h, 0:128], identity=ident[:, :]), r=[qa, ident], w=[psT])
                k.op("pe", lambda e: e.transpose(out=psQ[0:64, h, 1, :], in_=qa[:, h, 128:192], identity=ident[:, :]), r=[qa, ident], w=[psT])
            qT = qT_st.next()
            k.op("act", lambda e: e.activation(out=qT[:, :, 0, :], in_=psQ[:, :, 0, :], func=AF.Copy), r=[psT], w=[qT])
            k.op("act", lambda e: e.activation(out=qT[0:64, :, 1, :], in_=psQ[0:64, :, 1, :], func=AF.Copy), r=[psT], w=[qT])
            k.dma("pool", T["QT"], T["QT"][:, 0:128, t0:t0 + 128].rearrange("h d t -> d h t"), qT, qT[:, :, 0, :])
            k.dma("pool", T["QT"], T["QT"][:, 128:192, t0:t0 + 128].rearrange("h d t -> d h t"), qT, qT[0:64, :, 1, :])
            for h in range(4):
                k.op("pe", lambda e: e.transpose(out=psT[:, h, :], in_=kn[:, h, :], identity=ident[:, :]), r=[kn, ident], w=[psT])
            k.op("pe", lambda e: e.transpose(out=psT[0:64, 4, :], in_=krb[:, :], identity=ident[:, :]), r=[krb, ident], w=[psT])
            kT = kT_st.next()
            k.op("act", lambda e: e.activation(out=kT[:, 0:4, :], in_=psT[:, 0:4, :], func=AF.Copy), r=[psT], w=[kT])
            k.op("act", lambda e: e.activation(out=kT[0:64, 4, :], in_=psT[0:64, 4, :], func=AF.Copy), r=[psT], w=[kT])
            k.dma("pool", T["KT"], T["KT"][:, :, t0:t0 + 128].rearrange("h d t -> d h t"), kT, kT[:, 0:4, :])
            k.dma("pool", T["KRT"], T["KRT"][:, t0:t0 + 128], kT, kT[0:64, 4, :])
            for j3 in range(3):
                gs = g_st.next()
                for j2 in range(2):
                    j = j3 * 2 + j2
                    pa = psA.next()
                    tok_mm(pa, C_G + j * 512, 512)
                    k.op("act", lambda e: e.activation(out=gs[:, j2 * 512:(j2 + 1) * 512], in_=pa[:, :], func=AF.Sigmoid), r=[pa], w=[gs])
                k.dma("pool", T["gates"], T["gates"][t0:t0 + 128, j3 * 1024:(j3 + 1) * 1024], gs, gs[:, :])
        k.end_pass()


W_SPECS = {
    "w_in": lambda L: [L, 1024, IN_COLS], "w_uq": lambda L: [L, 384, 768], "w_ukv": lambda L: [L, 256, 1024],
    "w_branch": lambda L: [L, 1536, 1024], "w_out": lambda L: [L, 1024, 1024], "peer_wq": lambda L: [L, 1024, 2048],
    "keysT": lambda L: [L, 128, 16, 128], "peer_u": lambda L: [L, 16384, 1024], "peer_v": lambda L: [L, 16384, 1024],
    "mixg": lambda L: [L, 128, 1024], "ffng": lambda L: [L, 128, 1024], "sgn": lambda L: [L, 128, 512],
    "qag": lambda L: [L, 128, 384], "kvag": lambda L: [L, 128, 256], "qng": lambda L: [L, 128, 128],
    "qrg": lambda L: [L, 128, 64], "kng": lambda L: [L, 128, 128], "krg": lambda L: [L, 128, 64], "ong": lambda L: [L, 128, 128],
    "conv_wT": lambda L: [L, 128, 12, 5], "sg_wT": lambda L: [L, 128, 4, 128], "sg_bT": lambda L: [L, 128, 4],
    "dtb": lambda L: [L, 8, 1], "alog": lambda L: [L, 8, 1],
    "ident": lambda L: [128, 128], "masks": lambda L: [4, 128, 128], "sel": lambda L: [16, 16, 128],
    "rope": lambda L: [16384, 64], "rmask": lambda L: [8, 2048], "blend": lambda L: [8, 2], "iota": lambda L: [128, 256],
}


def make_cfg(seqs, depth, debug=()):
    NT = sum(seqs)
    tile_pos, tile_seq, seq_tiles = [], [], []
    t = 0
    for si, s in enumerate(seqs):
        assert s % 128 == 0
        seq_tiles.append((t, t + s // 128))
        for j in range(s // 128):
            tile_pos.append(j * 128)
            tile_seq.append(si)
            t += 1
    return dict(seqs=list(seqs), NT=NT, depth=depth, tile_pos=tile_pos, tile_seq=tile_seq, seq_tiles=seq_tiles, debug=tuple(debug))


def build(cfg, passes=("a", "g", "b", "c", "d")):
    L = cfg["depth"]
    NT = cfg["NT"]
    nc = bass.Bass("TRN2", target_bir_lowering=False)
    es = ExitStack()
    k = KB(nc, es)
    k.setup_engines()
    class LazyW(dict):
        def __missing__(self, name):
            b = k.dram(name, W_SPECS[name](L), F32, kind="ExternalInput")
            self[name] = b
            return b
    W = LazyW()
    x_in = k.dram("x", [NT, 1024], F32, kind="ExternalInput")
    y_out = k.dram("y", [NT, 1024], F32, kind="ExternalOutput")
    dbg = cfg["debug"]

    def scratch(name, shape, dt):
        return k.dram(name, shape, dt, kind=("ExternalOutput" if name in dbg else "Internal"))
    T = dict(W=W)
    T["qkvT"] = scratch("qkvT", [1536, NT], F32)
    T["abT"] = scratch("abT", [16, NT], F32)
    T["gbT"] = scratch("gbT", [16, NT], F32)
    T["zs"] = scratch("zs", [NT, 512], F32)
    T["ob"] = scratch("ob", [NT, 512], BF16)
    T["oaf"] = scratch("oaf", [NT, 512], F32)
    T["oa"] = scratch("oa", [NT, 512], BF16)
    T["V"] = scratch("V", [NT, 512], BF16)
    T["QT"] = scratch("QT", [4, 192, NT], BF16)
    T["KT"] = scratch("KT", [4, 128, NT], BF16)
    T["KRT"] = scratch("KRT", [64, NT], BF16)
    T["ocT"] = scratch("ocT", [512, NT], BF16)
    T["gates"] = scratch("gates", [NT, 3072], BF16)
    T["xmid"] = scratch("xmid", [NT, 1024], F32)
    outs = []
    xin = x_in
    for l in range(L):
        xout = y_out if l == L - 1 else T["xmid"]
        if "a" in passes:
            pass_a(k, cfg, T, l, xin)
        if "g" in passes:
            pass_g(k, cfg, T, l)
        if "b" in passes:
            pass_b(k, cfg, T, l)
        if "c" in passes:
            pass_c(k, cfg, T, l)
        if "d" in passes:
            pass_d(k, cfg, T, l, xin, xout)
        xin = xout
    final = [y_out] + [T[n] for n in dbg]
    k.final_wait("pool", final)
    es.close()
    k.used_inputs = list(W.keys())
    return nc, k


def host_weights(inp, L):
    f = np.float32
    rep = lambda a: np.ascontiguousarray(np.broadcast_to(np.asarray(a, f)[:, None, :], (L, 128, a.shape[-1])))
    Wd = {}
    Wd["w_in"] = np.asarray(inp["w_in"], f)
    Wd["w_uq"] = np.asarray(inp["w_uq"], f)
    Wd["w_ukv"] = np.asarray(inp["w_ukv"], f)
    Wd["w_branch"] = np.asarray(inp["w_branch"], f).reshape(L, 1536, 1024)
    Wd["w_out"] = np.asarray(inp["w_out"], f)
    Wd["peer_wq"] = np.asarray(inp["peer_wq"], f)
    Wd["keysT"] = np.ascontiguousarray(np.transpose(np.asarray(inp["peer_keys"], f), (0, 4, 1, 2, 3)).reshape(L, 128, 16, 128))
    Wd["peer_u"] = np.asarray(inp["peer_u"], f)
    Wd["peer_v"] = np.asarray(inp["peer_v"], f)
    Wd["mixg"] = rep(inp["mix_norm"]); Wd["ffng"] = rep(inp["ffn_norm"]); Wd["sgn"] = rep(inp["sg_norm"])
    Wd["qag"] = rep(inp["q_a_norm"]); Wd["kvag"] = rep(inp["kv_a_norm"]); Wd["qng"] = rep(inp["q_nope_norm"])
    Wd["qrg"] = rep(inp["q_rope_norm"]); Wd["kng"] = rep(inp["k_nope_norm"]); Wd["krg"] = rep(inp["k_rope_norm"])
    Wd["ong"] = rep(inp["o_norm"])
    Wd["conv_wT"] = np.ascontiguousarray(np.transpose(np.asarray(inp["conv_w"], f).reshape(L, 5, 12, 128), (0, 3, 2, 1)))
    Wd["sg_wT"] = np.ascontiguousarray(np.transpose(np.asarray(inp["sg_w"], f), (0, 3, 1, 2)))
    Wd["sg_bT"] = np.ascontiguousarray(np.transpose(np.asarray(inp["sg_b"], f), (0, 2, 1)))
    Wd["dtb"] = np.asarray(inp["dt_bias"], f).reshape(L, 8, 1)
    Wd["alog"] = np.asarray(inp["a_log"], f).reshape(L, 8, 1)
    Wd["ident"] = np.eye(128, dtype=f)
    i = np.arange(128)
    m = np.zeros((4, 128, 128), f)
    m[0] = (i[:, None] >= i[None, :]); m[1] = (i[:, None] > i[None, :])
    m[2] = (i[:, None] <= i[None, :]); m[3] = (i[:, None] < i[None, :])
    Wd["masks"] = m
    sel = np.zeros((16, 16, 128), f)
    for s in range(16):
        sel[s, s, :] = 1.0
    Wd["sel"] = sel
    inv_freq = (10000.0 ** (-np.arange(0, 64, 2, dtype=np.float32) / 64)).astype(f)
    ang = np.arange(16384, dtype=f)[:, None] * inv_freq[None, :]
    Wd["rope"] = np.concatenate([np.cos(ang), np.sin(ang)], axis=1).astype(f)
    rm = np.ones((8, 2048), f); rm[:, ::128] = 0.0
    Wd["rmask"] = rm
    bl = np.zeros((8, 2), f); bl[0:4, 0] = 1.0; bl[4:8, 1] = 1.0
    Wd["blend"] = bl
    Wd["iota"] = np.ascontiguousarray(np.broadcast_to(np.arange(256, dtype=f)[None, :], (128, 256)))
    return Wd


def pass_g(k, cfg, T, l):
    NT = cfg["NT"]
    W = T["W"]
    PW = 2048
    with ExitStack() as ps:
        k.begin_pass(ps)
        dtb = k.sbuf("dtb", [8, 1], F32)
        na = k.sbuf("na", [8, 1], F32)
        bl = k.sbuf("bl", [8, 2], F32)
        rmask = k.sbuf("rmask", [8, PW], F32)
        k.dma("sp", dtb, dtb[:, :], W["dtb"], W["dtb"][l])
        k.dma("sp", na, na[:, :], W["alog"], W["alog"][l])
        k.dma("sp", bl, bl[:, :], W["blend"], W["blend"][:, :])
        k.dma("sp", rmask, rmask[:, :], W["rmask"], W["rmask"][:, :])
        k.op("act", lambda e: e.activation(out=na[:, :], in_=na[:, :], func=AF.Exp), r=[na], w=[na])
        k.op("dve", lambda e: e.tensor_scalar(out=na[:, :], in0=na[:, :], scalar1=-1.0, scalar2=None, op0=ALU.mult), r=[na], w=[na])
        al = k.sbuf("al", [8, PW], F32)
        be = k.sbuf("be", [8, PW], F32)
        g = k.sbuf("g", [8, PW], F32)
        pre = k.sbuf("pre", [8, PW], F32)
        suf = k.sbuf("suf", [8, PW], F32)
        gam = k.sbuf("gam", [8, PW], F32)
        for c0 in range(0, NT, PW):
            n = min(PW, NT - c0)
            nch = n // 128
            k.dma("sp", al, al[:, 0:n], T["abT"], T["abT"][0:8, c0:c0 + n])
            k.dma("sp", be, be[:, 0:n], T["abT"], T["abT"][8:16, c0:c0 + n])
            k.op("act", lambda e: e.activation(out=g[:, 0:n], in_=al[:, 0:n], func=AF.Exp, bias=dtb[:, 0:1]), r=[al, dtb], w=[g])
            k.op("act", lambda e: e.activation(out=g[:, 0:n], in_=g[:, 0:n], func=AF.Ln, bias=1.0), r=[g], w=[g])
            k.op("dve", lambda e: e.tensor_scalar(out=g[:, 0:n], in0=g[:, 0:n], scalar1=na[:, 0:1], scalar2=None, op0=ALU.mult), r=[g, na], w=[g])
            k.op("dve", lambda e: e.tensor_tensor_scan(out=pre[:, 0:n], data0=rmask[:, 0:n], data1=g[:, 0:n], initial=0.0, op0=ALU.mult, op1=ALU.add), r=[rmask, g], w=[pre])
            k.op("dve", lambda e: e.tensor_tensor(out=suf[:, 0:n], in0=g[:, 0:n], in1=pre[:, 0:n], op=ALU.subtract), r=[g, pre], w=[suf])
            pre3 = pre[:, 0:n].rearrange("p (a b) -> p a b", b=128)
            suf3 = suf[:, 0:n].rearrange("p (a b) -> p a b", b=128)
            k.op("dve", lambda e: e.tensor_tensor(out=suf3, in0=suf3, in1=pre3[:, :, 127:128].to_broadcast([8, nch, 128]), op=ALU.add), r=[suf, pre], w=[suf])
            k.op("dve", lambda e: e.tensor_scalar(out=gam[:, 0:n], in0=pre[:, 0:n], scalar1=bl[:, 0:1], scalar2=None, op0=ALU.mult), r=[pre, bl], w=[gam])
            k.op("dve", lambda e: e.scalar_tensor_tensor(out=gam[:, 0:n], in0=suf[:, 0:n], scalar=bl[:, 1:2], in1=gam[:, 0:n], op0=ALU.mult, op1=ALU.add), r=[suf, bl, gam], w=[gam])
            k.op("act", lambda e: e.activation(out=be[:, 0:n], in_=be[:, 0:n], func=AF.Sigmoid), r=[be], w=[be])
            k.dma("pool", T["gbT"], T["gbT"][0:8, c0:c0 + n], gam, gam[:, 0:n])
            k.dma("pool", T["gbT"], T["gbT"][8:16, c0:c0 + n], be, be[:, 0:n])
        k.end_pass()


def pass_b(k, cfg, T, l):
    NT = cfg["NT"]
    W = T["W"]
    with ExitStack() as ps:
        k.begin_pass(ps)
        cw = k.sbuf("cw", [128, 12, 5], F32)
        k.dma("sp", cw, cw[:, :, :], W["conv_wT"], W["conv_wT"][l])
        identf = k.sbuf("identf", [128, 128], F32)
        k.dma("sp", identf, identf[:, :], W["ident"], W["ident"][:, :])
        masks = k.sbuf("masks", [128, 4, 128], F32)
        k.dma("sp", masks, masks[:, :, :], W["masks"], W["masks"].rearrange("m p j -> p m j"))
        sel = k.sbuf("sel", [16, 16, 128], F32)
        k.dma("sp", sel, sel[:, :, :], W["sel"], W["sel"][:, :, :])
        ong = k.sbuf("ong", [128, 128], F32)
        k.dma("sp", ong, ong[:, :], W["ong"], W["ong"][l])
        ones = k.sbuf("ones", [128, 128], F32)
        k.op("dve", lambda e: e.memset(ones[:, :], 1.0), w=[ones])

        X3 = Rot([k.sbuf("X3_%d" % i, [128, 3, 132], F32) for i in range(2)])
        Y = k.sbuf("Y", [128, 3, 128], F32)
        SQ = k.sbuf("SQ", [128, 256], F32)
        Rn = k.sbuf("Rn", [128, 256], F32)
        QT = k.sbuf("QTb", [128, 128], F32)
        KT = k.sbuf("KTb", [128, 128], F32)
        Km = k.sbuf("Km", [128, 128], F32)
        Vm = k.sbuf("Vm", [128, 128], F32)
        Ar = k.sbuf("Ar", [128, 128], F32)
        QKr = k.sbuf("QKr", [128, 128], F32)
        GB = Rot([k.sbuf("GB%d" % i, [16, 128], F32) for i in range(2)])
        GBc = k.sbuf("GBc", [128, 16], F32)
        NB = k.sbuf("NB", [128, 8], F32)
        EGc = k.sbuf("EGc", [128, 8], F32)
        KBE = k.sbuf("KBE", [128, 8], F32)
        G2s = k.sbuf("G2s", [128, 128], F32)
        Dm = k.sbuf("Dm", [128, 128], F32)
        DT = k.sbuf("DT", [128, 128], F32)
        Bs = [k.sbuf("Bs%d" % i, [128, 128], F32) for i in range(2)]
        BTs = [k.sbuf("BTs%d" % i, [128, 128], F32) for i in range(2)]
        UW = k.sbuf("UW", [128, 256], F32)
        wT = k.sbuf("wT", [128, 128], F32)
        P2T = k.sbuf("P2T", [128, 128], F32)
        qdT = k.sbuf("qdT", [128, 128], F32)
        kd = k.sbuf("kd", [128, 128], F32)
        sc = k.sbuf("sc", [128, 4], F32)
        vnew = k.sbuf("vnew", [128, 128], F32)
        S = [k.sbuf("S%d" % h, [128, 128], F32) for h in range(4)]
        ost = Rot([k.sbuf("ost%d" % i, [128, 4, 128], F32) for i in range(2)])
        oin = Rot([k.sbuf("oin%d" % i, [128, 4, 128], F32) for i in range(2)])
        zin = Rot([k.sbuf("zin%d" % i, [128, 4, 128], F32) for i in range(2)])
        osum = k.sbuf("osum", [128, 128], F32)
        junk = k.sbuf("junkb", [128, 128], F32)
        oab = Rot([k.sbuf("oab%d" % i, [128, 4, 128], BF16) for i in range(2)])
        pp = Rot([k.psum("ppb%d" % i, [128, 512], F32) for i in range(8)])

        for dr in range(2):
            for (ta, tb) in cfg["seq_tiles"]:
                tiles = list(range(ta, tb)) if dr == 0 else list(range(tb - 1, ta - 1, -1))
                for h in range(4):
                    k.op("pool", lambda e: e.memset(S[h][:, :], 0.0), w=[S[h]])
                for t in tiles:
                    t0 = t * 128
                    gb = GB.next()
                    k.dma("sp", gb, gb[:, :], T["gbT"], T["gbT"][:, t0:t0 + 128])
                    p = pp.next()
                    k.op("pe", lambda e: e.transpose(out=p[:, 0:16], in_=gb[:, :], identity=identf[0:16, 0:16]), r=[gb, identf], w=[p])
                    k.op("act", lambda e: e.activation(out=GBc[:, :], in_=p[:, 0:16], func=AF.Copy), r=[p], w=[GBc])
                    k.op("dve", lambda e: e.tensor_scalar(out=NB[:, :], in0=GBc[:, 8:16], scalar1=-1.0, scalar2=None, op0=ALU.mult), r=[GBc], w=[NB])
                    k.op("act", lambda e: e.activation(out=EGc[:, :], in_=GBc[:, 0:8], func=AF.Exp), r=[GBc], w=[EGc])
                    k.op("dve", lambda e: e.tensor_tensor(out=KBE[:, :], in0=EGc[:, :], in1=GBc[:, 8:16], op=ALU.mult), r=[EGc, GBc], w=[KBE])
                    if dr == 1:
                        oi = oin.next()
                        zi = zin.next()
                        k.dma("sp", oi, oi[:, :, :], T["oaf"], T["oaf"][t0:t0 + 128, :].rearrange("p (a b) -> p a b", a=4))
                        k.dma("sp", zi, zi[:, :, :], T["zs"], T["zs"][t0:t0 + 128, :].rearrange("p (a b) -> p a b", a=4))
                        ob_ = oab.next()
                    else:
                        os_ = ost.next()
                    for h in range(4):
                        r = dr * 4 + h
                        x3 = X3.next()
                        lo = t0 - 2
                        hi = t0 + 130
                        c_lo = 0
                        c_hi = 132
                        if t == ta:
                            lo = t0
                            c_lo = 2
                        if t == tb - 1:
                            hi = t0 + 128
                            c_hi = 130
                        if c_lo > 0 or c_hi < 132:
                            k.op("pool", lambda e: e.memset(x3[:, :, :], 0.0), w=[x3])
                        for j in range(3):
                            c = 4 * j + h
                            k.dma("sp", x3, x3[:, j, c_lo:c_hi], T["qkvT"], T["qkvT"][c * 128:(c + 1) * 128, lo:hi])
                        for j in range(3):
                            c = 4 * j + h
                            k.op("dve", lambda e: e.tensor_scalar(out=Y[:, j, :], in0=x3[:, j, 0:128], scalar1=cw[:, c, 0:1], scalar2=None, op0=ALU.mult), r=[x3, cw], w=[Y])
                            for tap in range(1, 5):
                                k.op("dve", lambda e: e.scalar_tensor_tensor(out=Y[:, j, :], in0=x3[:, j, tap:tap + 128], scalar=cw[:, c, tap:tap + 1], in1=Y[:, j, :], op0=ALU.mult, op1=ALU.add), r=[x3, cw, Y], w=[Y])
                        k.op("act", lambda e: e.activation(out=Y[:, :, :], in_=Y[:, :, :], func=AF.Silu), r=[Y], w=[Y])
                        k.op("dve", lambda e: e.tensor_tensor(out=SQ[:, :], in0=Y[:, 0:2, :].rearrange("p a b -> p (a b)"), in1=Y[:, 0:2, :].rearrange("p a b -> p (a b)"), op=ALU.mult), r=[Y], w=[SQ])
                        p = pp.next()
                        k.op("pe", lambda e: e.matmul(p[:, 0:256], lhsT=ones[:, :], rhs=SQ[:, :], start=True, stop=True), r=[ones, SQ], w=[p])
                        k.op("dve", lambda e: e.tensor_scalar(out=Rn[:, :], in0=p[:, 0:256], scalar1=EPS, scalar2=None, op0=ALU.add), r=[p], w=[Rn])
                        k.op("act", lambda e: e.activation(out=Rn[:, :], in_=Rn[:, :], func=AF.Sqrt), r=[Rn], w=[Rn])
                        k.op("dve", lambda e: e.reciprocal(out=Rn[:, :], in_=Rn[:, :]), r=[Rn], w=[Rn])
                        k.op("dve", lambda e: e.scalar_tensor_tensor(out=QT[:, :], in0=Y[:, 0, :], scalar=128 ** -0.5, in1=Rn[:, 0:128], op0=ALU.mult, op1=ALU.mult), r=[Y, Rn], w=[QT])
                        k.op("dve", lambda e: e.tensor_tensor(out=KT[:, :], in0=Y[:, 1, :], in1=Rn[:, 128:256], op=ALU.mult), r=[Y, Rn], w=[KT])
                        p = pp.next()
                        k.op("pe", lambda e: e.transpose(out=p[:, 0:128], in_=KT[:, :], identity=identf[:, :]), r=[KT, identf], w=[p])
                        k.op("pe", lambda e: e.transpose(out=p[:, 128:256], in_=Y[:, 2, :], identity=identf[:, :]), r=[Y, identf], w=[p])
                        k.op("act", lambda e: e.activation(out=Km[:, :], in_=p[:, 0:128], func=AF.Copy), r=[p], w=[Km])
                        k.op("act", lambda e: e.activation(out=Vm[:, :], in_=p[:, 128:256], func=AF.Copy), r=[p], w=[Vm])
                        p = pp.next()
                        k.op("pe", lambda e: e.matmul(p[:, 0:128], lhsT=KT[:, :], rhs=KT[:, :], start=True, stop=True), r=[KT], w=[p])
                        k.op("pe", lambda e: e.matmul(p[:, 128:256], lhsT=KT[:, :], rhs=QT[:, :], start=True, stop=True), r=[KT, QT], w=[p])
                        k.op("act", lambda e: e.activation(out=Ar[:, :], in_=p[:, 0:128], func=AF.Copy), r=[p], w=[Ar])
                        k.op("act", lambda e: e.activation(out=QKr[:, :], in_=p[:, 128:256], func=AF.Copy), r=[p], w=[QKr])
                        p = pp.next()
                        k.op("pe", lambda e: e.matmul(p[:, 0:128], lhsT=sel[:, r, :], rhs=gb[:, :], start=True, stop=True), r=[sel, gb], w=[p])
                        k.op("act", lambda e: e.activation(out=G2s[:, :], in_=p[:, 0:128], func=AF.Copy), r=[p], w=[G2s])
                        k.op("dve", lambda e: e.tensor_scalar(out=Dm[:, :], in0=G2s[:, :], scalar1=GBc[:, r:r + 1], scalar2=0.0, op0=ALU.subtract, op1=ALU.max), r=[G2s, GBc], w=[Dm])
                        k.op("act", lambda e: e.activation(out=Dm[:, :], in_=Dm[:, :], func=AF.Exp, scale=-1.0), r=[Dm], w=[Dm])
                        k.op("dve", lambda e: e.tensor_scalar(out=DT[:, :], in0=G2s[:, :], scalar1=GBc[:, r:r + 1], scalar2=0.0, op0=ALU.subtract, op1=ALU.min), r=[G2s, GBc], w=[DT])
                        k.op("act", lambda e: e.activation(out=DT[:, :], in_=DT[:, :], func=AF.Exp), r=[DT], w=[DT])
                        m_strict = 1 if dr == 0 else 3
                        m_inclT = 2 if dr == 0 else 0
                        k.op("dve", lambda e: e.tensor_tensor(out=Dm[:, :], in0=Dm[:, :], in1=masks[:, m_strict, :], op=ALU.mult), r=[Dm, masks], w=[Dm])
                        B, BT = Bs[0], BTs[0]
                        k.op("dve", lambda e: e.scalar_tensor_tensor(out=B[:, :], in0=Ar[:, :], scalar=NB[:, r:r + 1], in1=Dm[:, :], op0=ALU.mult, op1=ALU.mult), r=[Ar, NB, Dm], w=[B])
                        p = pp.next()
                        k.op("pe", lambda e: e.transpose(out=p[:, 0:128], in_=B[:, :], identity=identf[:, :]), r=[B, identf], w=[p])
                        k.op("act", lambda e: e.activation(out=BT[:, :], in_=p[:, 0:128], func=AF.Copy), r=[p], w=[BT])
                        k.op("dve", lambda e: e.tensor_scalar(out=UW[:, 0:128], in0=Vm[:, :], scalar1=GBc[:, 8 + r:9 + r], scalar2=None, op0=ALU.mult), r=[Vm, GBc], w=[UW])
                        k.op("dve", lambda e: e.tensor_scalar(out=UW[:, 128:256], in0=Km[:, :], scalar1=KBE[:, r:r + 1], scalar2=None, op0=ALU.mult), r=[Km, KBE], w=[UW])
                        for lev in range(7):
                            p = pp.next()
                            k.op("pe", lambda e: e.matmul(p[:, 0:256], lhsT=BT[:, :], rhs=UW[:, :], start=True, stop=True), r=[BT, UW], w=[p])
                            k.op("dve", lambda e: e.tensor_tensor(out=UW[:, :], in0=UW[:, :], in1=p[:, 0:256], op=ALU.add), r=[UW, p], w=[UW])
                            if lev < 6:
                                B2, BT2 = Bs[(lev + 1) % 2], BTs[(lev + 1) % 2]
                                p = pp.next()
                                k.op("pe", lambda e: e.matmul(p[:, 0:128], lhsT=BT[:, :], rhs=B[:, :], start=True, stop=True), r=[BT, B], w=[p])
                                k.op("pe", lambda e: e.matmul(p[:, 128:256], lhsT=B[:, :], rhs=BT[:, :], start=True, stop=True), r=[BT, B], w=[p])
                                k.op("act", lambda e: e.activation(out=B2[:, :], in_=p[:, 0:128], func=AF.Copy), r=[p], w=[B2])
                                k.op("act", lambda e: e.activation(out=BT2[:, :], in_=p[:, 128:256], func=AF.Copy), r=[p], w=[BT2])
                                B, BT = B2, BT2
                        p = pp.next()
                        k.op("pe", lambda e: e.transpose(out=p[:, 0:128], in_=UW[:, 128:256], identity=identf[:, :]), r=[UW, identf], w=[p])
                        k.op("act", lambda e: e.activation(out=wT[:, :], in_=p[:, 0:128], func=AF.Copy), r=[p], w=[wT])
                        k.op("dve", lambda e: e.tensor_tensor(out=DT[:, :], in0=DT[:, :], in1=masks[:, m_inclT, :], op=ALU.mult), r=[DT, masks], w=[DT])
                        k.op("dve", lambda e: e.tensor_tensor(out=P2T[:, :], in0=DT[:, :], in1=QKr[:, :], op=ALU.mult), r=[DT, QKr], w=[P2T])
                        k.op("act", lambda e: e.activation(out=qdT[:, :], in_=G2s[:, :], func=AF.Exp), r=[G2s], w=[qdT])
                        k.op("dve", lambda e: e.tensor_tensor(out=qdT[:, :], in0=qdT[:, :], in1=QT[:, :], op=ALU.mult), r=[qdT, QT], w=[qdT])
                        jt = 127 if dr == 0 else 0
                        k.op("act", lambda e: e.activation(out=sc[:, 0:1], in_=GBc[:, r:r + 1], func=AF.Exp, scale=-1.0, bias=G2s[:, jt:jt + 1]), r=[GBc, G2s], w=[sc])
                        k.op("act", lambda e: e.activation(out=sc[:, 1:2], in_=G2s[:, jt:jt + 1], func=AF.Exp), r=[G2s], w=[sc])
                        k.op("dve", lambda e: e.tensor_scalar(out=kd[:, :], in0=Km[:, :], scalar1=sc[:, 0:1], scalar2=None, op0=ALU.mult), r=[Km, sc], w=[kd])
                        Sh = S[h]
                        p = pp.next()
                        k.op("pe", lambda e: e.matmul(p[:, 0:128], lhsT=wT[:, :], rhs=Sh[:, :], start=True, stop=True), r=[wT, Sh], w=[p])
                        k.op("dve", lambda e: e.tensor_tensor(out=vnew[:, :], in0=UW[:, 0:128], in1=p[:, 0:128], op=ALU.subtract), r=[UW, p], w=[vnew])
                        p = pp.next()
                        k.op("pe", lambda e: e.matmul(p[:, 0:128], lhsT=qdT[:, :], rhs=Sh[:, :], start=True, stop=False), r=[qdT, Sh], w=[p])
                        k.op("pe", lambda e: e.matmul(p[:, 0:128], lhsT=P2T[:, :], rhs=vnew[:, :], start=False, stop=True), r=[P2T, vnew], w=[p])
                        k.op("pe", lambda e: e.matmul(p[:, 128:256], lhsT=kd[:, :], rhs=vnew[:, :], start=True, stop=True), r=[kd, vnew], w=[p])
                        if dr == 0:
                            k.op("act", lambda e: e.activation(out=os_[:, h, :], in_=p[:, 0:128], func=AF.Copy), r=[p], w=[os_])
                        else:
                            k.op("dve", lambda e: e.tensor_tensor(out=osum[:, :], in0=oi[:, h, :], in1=p[:, 0:128], op=ALU.add), r=[oi, p], w=[osum])
                            k.op("act", lambda e: e.activation(out=junk[:, :], in_=osum[:, :], func=AF.Square, accum_out=sc[:, 2:3]), r=[osum], w=[junk, sc])
                            k.op("dve", lambda e: e.tensor_scalar(out=sc[:, 3:4], in0=sc[:, 2:3], scalar1=1.0 / 128, scalar2=EPS, op0=ALU.mult, op1=ALU.add), r=[sc], w=[sc])
                            k.op("act", lambda e: e.activation(out=sc[:, 3:4], in_=sc[:, 3:4], func=AF.Sqrt), r=[sc], w=[sc])
                            k.op("dve", lambda e: e.reciprocal(out=sc[:, 3:4], in_=sc[:, 3:4]), r=[sc], w=[sc])
                            k.op("dve", lambda e: e.scalar_tensor_tensor(out=osum[:, :], in0=osum[:, :], scalar=sc[:, 3:4], in1=ong[:, :], op0=ALU.mult, op1=ALU.mult), r=[osum, sc, ong], w=[osum])
                            k.op("dve", lambda e: e.tensor_tensor(out=ob_[:, h, :], in0=osum[:, :], in1=zi[:, h, :], op=ALU.mult), r=[osum, zi], w=[ob_])
                        k.op("dve", lambda e: e.scalar_tensor_tensor(out=Sh[:, :], in0=Sh[:, :], scalar=sc[:, 1:2], in1=p[:, 128:256], op0=ALU.mult, op1=ALU.add), r=[Sh, sc, p], w=[Sh])
                    if dr == 0:
                        k.dma("pool", T["oaf"], T["oaf"][t0:t0 + 128, :].rearrange("p (a b) -> p a b", a=4), os_, os_[:, :, :])
                    else:
                        k.dma("pool", T["oa"], T["oa"][t0:t0 + 128, :].rearrange("p (a b) -> p a b", a=4), ob_, ob_[:, :, :])
        k.end_pass()


SHIFT = 4.0


def pass_c(k, cfg, T, l):
    with ExitStack() as ps:
        k.begin_pass(ps)
        smax = max(cfg["seqs"])
        KTs = k.sbuf("KTs", [128, smax], BF16)
        KRs = k.sbuf("KRs", [64, smax], BF16)
        Vs = k.sbuf("Vs", [128, smax // 128, 128], BF16)
        ones = k.sbuf("ones_c", [128, 128], BF16)
        k.op("dve", lambda e: e.memset(ones[:, :], 1.0), w=[ones])
        nsh = k.sbuf("nsh", [128, 1], F32)
        k.op("dve", lambda e: e.memset(nsh[:, :], -SHIFT), w=[nsh])
        Qn = Rot([k.sbuf("Qn%d" % i, [128, 512], BF16) for i in range(2)])
        Qr = Rot([k.sbuf("Qr%d" % i, [64, 512], BF16) for i in range(2)])
        PT = Rot([k.sbuf("PT%d" % i, [128, 512], BF16) for i in range(3)])
        rl = k.sbuf("rl", [128, 512], F32)
        oc = Rot([k.sbuf("oc%d" % i, [128, 512], BF16) for i in range(2)])
        psS = Rot([k.psum("psS%d" % i, [128, 512], F32) for i in range(3)])
        psO = Rot([k.psum("psO%d" % i, [128, 512], F32) for i in range(2)])
        psL = Rot([k.psum("psL%d" % i, [128, 512], F32) for i in range(2)])
        for (ta, tb) in cfg["seq_tiles"]:
            s0 = ta * 128
            sl = (tb - ta) * 128
            nkb = tb - ta
            for h in range(4):
                k.dma("sp", KTs, KTs[:, 0:sl], T["KT"], T["KT"][h, :, s0:s0 + sl])
                k.dma("sp", KRs, KRs[:, 0:sl], T["KRT"], T["KRT"][:, s0:s0 + sl])
                k.dma("sp", Vs, Vs[:, 0:nkb, :], T["V"], T["V"][s0:s0 + sl, h * 128:(h + 1) * 128].rearrange("(a p) d -> p a d", p=128))
                for q0 in range(0, sl, 512):
                    qb = min(512, sl - q0)
                    qn = Qn.next()
                    qr = Qr.next()
                    k.dma("sp", qn, qn[:, 0:qb], T["QT"], T["QT"][h, 0:128, s0 + q0:s0 + q0 + qb])
                    k.dma("sp", qr, qr[:, 0:qb], T["QT"], T["QT"][h, 128:192, s0 + q0:s0 + q0 + qb])
                    po = psO.next()
                    pl = psL.next()
                    for kb in range(nkb):
                        pS = psS.next()
                        k.op("pe", lambda e: e.matmul(pS[:, 0:qb], lhsT=KTs[:, kb * 128:(kb + 1) * 128], rhs=qn[:, 0:qb], start=True, stop=False), r=[KTs, qn], w=[pS])
                        k.op("pe", lambda e: e.matmul(pS[:, 0:qb], lhsT=KRs[:, kb * 128:(kb + 1) * 128], rhs=qr[:, 0:qb], start=False, stop=True), r=[KRs, qr], w=[pS])
                        pt = PT.next()
                        k.op("act", lambda e: e.activation(out=pt[:, 0:qb], in_=pS[:, 0:qb], func=AF.Exp, scale=SM_SCALE, bias=nsh[:, 0:1]), r=[pS, nsh], w=[pt])
                        k.op("pe", lambda e: e.matmul(po[:, 0:qb], lhsT=Vs[:, kb, :], rhs=pt[:, 0:qb], start=(kb == 0), stop=(kb == nkb - 1)), r=[Vs, pt], w=[po])
                        k.op("pe", lambda e: e.matmul(pl[:, 0:qb], lhsT=ones[:, :], rhs=pt[:, 0:qb], start=(kb == 0), stop=(kb == nkb - 1)), r=[ones, pt], w=[pl])
                    k.op("dve", lambda e: e.reciprocal(out=rl[:, 0:qb], in_=pl[:, 0:qb]), r=[pl], w=[rl])
                    o = oc.next()
                    k.op("dve", lambda e: e.tensor_tensor(out=o[:, 0:qb], in0=po[:, 0:qb], in1=rl[:, 0:qb], op=ALU.mult), r=[po, rl], w=[o])
                    k.dma("pool", T["ocT"], T["ocT"][h * 128:(h + 1) * 128, s0 + q0:s0 + q0 + qb], o, o[:, 0:qb])
        k.end_pass()


def top16(k, src, src_ap, work, work_ap, mx, mx_ap8a, mx_ap8b, ix, ix_ap8a, ix_ap8b):
    k.op("dve", lambda e: e.max(out=mx_ap8a, in_=src_ap), r=[src], w=[mx])
    k.op("dve", lambda e: e.max_index(out=ix_ap8a, in_max=mx_ap8a, in_values=src_ap), r=[src, mx], w=[ix])
    k.op("dve", lambda e: e.match_replace(out=work_ap, in_to_replace=mx_ap8a, in_values=src_ap, imm_value=-1e30), r=[src, mx], w=[work])
    k.op("dve", lambda e: e.max(out=mx_ap8b, in_=work_ap), r=[work], w=[mx])
    k.op("dve", lambda e: e.max_index(out=ix_ap8b, in_max=mx_ap8b, in_values=work_ap), r=[work, mx], w=[ix])


def pass_d(k, cfg, T, l, xin, xout):
    NT = cfg["NT"]
    ntile = NT // 128
    W = T["W"]
    with ExitStack() as ps:
        k.begin_pass(ps)
        wbr = k.sbuf("wbr", [128, 12, 1024], BF16)
        wout = k.sbuf("wout", [128, 8, 1024], BF16)
        wq = k.sbuf("wq", [128, 8, 2048], BF16)
        keysT = k.sbuf("keysT", [128, 16, 128], BF16)
        stage = Rot([k.sbuf("stgd%d" % i, [128, 2048], F32) for i in range(1)])
        for kc in range(12):
            load_cast(k, "sp", wbr, wbr[:, kc, :], W["w_branch"], W["w_branch"][l, kc * 128:(kc + 1) * 128, :], stage, (128, 1024))
        for kc in range(8):
            load_cast(k, "sp", wout, wout[:, kc, :], W["w_out"], W["w_out"][l, kc * 128:(kc + 1) * 128, :], stage, (128, 1024))
            load_cast(k, "sp", wq, wq[:, kc, :], W["peer_wq"], W["peer_wq"][l, kc * 128:(kc + 1) * 128, :], stage, (128, 2048))
        load_cast(k, "sp", keysT, keysT[:, :, :].rearrange("p a b -> p (a b)"), W["keysT"], W["keysT"][l].rearrange("p a b -> p (a b)"), stage, (128, 2048))
        ffng = k.sbuf("ffng", [128, 1024], F32)
        k.dma("sp", ffng, ffng[:, :], W["ffng"], W["ffng"][l])
        identf = k.sbuf("identf_d", [128, 128], F32)
        k.dma("sp", identf, identf[:, :], W["ident"], W["ident"][:, :])
        ident = k.sbuf("ident_d", [128, 128], BF16)
        k.op("dve", lambda e: e.tensor_copy(out=ident[:, :], in_=identf[:, :]), r=[identf], w=[ident])
        iota = k.sbuf("iota", [128, 256], F32)
        k.dma("sp", iota, iota[:, :], W["iota"], W["iota"][:, :])

        oab = k.sbuf("oab_d", [128, 1024], BF16)
        XT = k.sbuf("XT", [128, 12, 128], BF16)
        gt = k.sbuf("gt", [128, 3072], BF16)
        xt = k.sbuf("xt_d", [128, 1024], F32)
        mg = k.sbuf("mg", [128, 1024], F32)
        tm = k.sbuf("tm", [128, 512], F32)
        mgb = k.sbuf("mgb", [128, 1024], BF16)
        mT = k.sbuf("mT", [128, 8, 128], BF16)
        hh = k.sbuf("hh", [128, 1024], F32)
        hn = k.sbuf("hn", [128, 1024], F32)
        hnb = k.sbuf("hnb", [128, 1024], BF16)
        hnT = k.sbuf("hnT", [128, 8, 128], BF16)
        junk = k.sbuf("junk_d", [128, 1024], F32)
        st = k.sbuf("st_d", [128, 8], F32)
        qT = k.sbuf("qT_d", [128, 16, 128], BF16)
        sc = k.sbuf("sc_d", [128, 16, 128], F32)
        wk = k.sbuf("wk_d", [128, 256], F32)
        tv = k.sbuf("tv", [128, 16, 16], F32)
        ti = k.sbuf("ti", [128, 16, 16], U32)
        tif = k.sbuf("tif", [128, 16, 16], F32)
        cand = k.sbuf("cand", [128, 8, 256], F32)
        bv = k.sbuf("bv", [128, 8, 16], F32)
        bp = k.sbuf("bp", [128, 8, 16], U32)
        pa_i = k.sbuf("pa_i", [128, 8, 16], I32)
        paf = k.sbuf("paf", [128, 8, 16], F32)
        pbf = k.sbuf("pbf", [128, 8, 16], F32)
        eq = k.sbuf("eq", [128, 8, 16, 16], F32)
        i0 = k.sbuf("i0", [128, 8, 16], F32)
        i1 = k.sbuf("i1", [128, 8, 16], F32)
        idx = k.sbuf("idx", [128, 128], U32)
        gw = k.sbuf("gw", [128, 8, 16], F32)
        hd = k.sbuf("hd", [128, 128], F32)
        gx = k.sbuf("gx", [128, 128], F32)
        gtmp = k.sbuf("gtmp", [128, 128], F32)
        wgt = k.sbuf("wgt", [128, 128], F32)
        acc = k.sbuf("acc", [128, 1024], F32)
        yb = Rot([k.sbuf("yb%d" % i, [128, 1024], F32) for i in range(1)])
        gbuf = Rot([k.sbuf("gbuf%d" % i, [128, 1024], F32) for i in range(4)])
        psT = k.psum("psT_d", [128, 8, 128], BF16)
        psM = Rot([k.psum("psM%d" % i, [128, 512], F32) for i in range(4)])

        for t in range(ntile):
            t0 = t * 128
            k.dma("sp", oab, oab[:, 0:512], T["oa"], T["oa"][t0:t0 + 128, :])
            k.dma("sp", oab, oab[:, 512:1024], T["ob"], T["ob"][t0:t0 + 128, :])
            k.dma("sp", XT, XT[:, 8:12, :], T["ocT"], T["ocT"][:, t0:t0 + 128].rearrange("(c p) t -> p c t", p=128))
            k.dma("sp", gt, gt[:, :], T["gates"], T["gates"][t0:t0 + 128, :])
            k.dma("sp", xt, xt[:, :], xin, xin[t0:t0 + 128, :])
            for c in range(8):
                k.op("pe", lambda e: e.transpose(out=psT[:, c, :], in_=oab[:, c * 128:(c + 1) * 128], identity=ident[:, :]), r=[oab, ident], w=[psT])
            k.op("act", lambda e: e.activation(out=XT[:, 0:8, :], in_=psT[:, :, :], func=AF.Copy), r=[psT], w=[XT])
            for hf in range(2):
                for b in range(3):
                    pm = psM.next()
                    for c in range(4):
                        k.op("pe", lambda e: e.matmul(pm[:, :], lhsT=XT[:, b * 4 + c, :], rhs=wbr[:, b * 4 + c, hf * 512:(hf + 1) * 512], start=(c == 0), stop=(c == 3)), r=[XT, wbr], w=[pm])
                    gsl = gt[:, b * 1024 + hf * 512:b * 1024 + (hf + 1) * 512]
                    if b == 0:
                        k.op("dve", lambda e: e.tensor_tensor(out=mg[:, hf * 512:(hf + 1) * 512], in0=pm[:, :], in1=gsl, op=ALU.mult), r=[pm, gt], w=[mg])
                    else:
                        k.op("dve", lambda e: e.tensor_tensor(out=tm[:, :], in0=pm[:, :], in1=gsl, op=ALU.mult), r=[pm, gt], w=[tm])
                        k.op("dve", lambda e: e.tensor_tensor(out=mg[:, hf * 512:(hf + 1) * 512], in0=mg[:, hf * 512:(hf + 1) * 512], in1=tm[:, :], op=ALU.add), r=[mg, tm], w=[mg])
            k.op("act", lambda e: e.activation(out=mgb[:, :], in_=mg[:, :], func=AF.Copy), r=[mg], w=[mgb])
            for c in range(8):
                k.op("pe", lambda e: e.transpose(out=psT[:, c, :], in_=mgb[:, c * 128:(c + 1) * 128], identity=ident[:, :]), r=[mgb, ident], w=[psT])
            k.op("act", lambda e: e.activation(out=mT[:, :, :], in_=psT[:, :, :], func=AF.Copy), r=[psT], w=[mT])
            for hf in range(2):
                pm = psM.next()
                for c in range(8):
                    k.op("pe", lambda e: e.matmul(pm[:, :], lhsT=mT[:, c, :], rhs=wout[:, c, hf * 512:(hf + 1) * 512], start=(c == 0), stop=(c == 7)), r=[mT, wout], w=[pm])
                k.op("dve", lambda e: e.tensor_tensor(out=hh[:, hf * 512:(hf + 1) * 512], in0=pm[:, :], in1=xt[:, hf * 512:(hf + 1) * 512], op=ALU.add), r=[pm, xt], w=[hh])
            k.op("act", lambda e: e.activation(out=junk[:, :], in_=hh[:, :], func=AF.Square, accum_out=st[:, 0:1]), r=[hh], w=[junk, st])
            k.op("dve", lambda e: e.tensor_scalar(out=st[:, 1:2], in0=st[:, 0:1], scalar1=1.0 / 1024, scalar2=EPS, op0=ALU.mult, op1=ALU.add), r=[st], w=[st])
            k.op("act", lambda e: e.activation(out=st[:, 1:2], in_=st[:, 1:2], func=AF.Sqrt), r=[st], w=[st])
            k.op("dve", lambda e: e.reciprocal(out=st[:, 1:2], in_=st[:, 1:2]), r=[st], w=[st])
            k.op("dve", lambda e: e.scalar_tensor_tensor(out=hn[:, :], in0=hh[:, :], scalar=st[:, 1:2], in1=ffng[:, :], op0=ALU.mult, op1=ALU.mult), r=[hh, st, ffng], w=[hn])
            k.op("act", lambda e: e.activation(out=hnb[:, :], in_=hn[:, :], func=AF.Copy), r=[hn], w=[hnb])
            for c in range(8):
                k.op("pe", lambda e: e.transpose(out=psT[:, c, :], in_=hnb[:, c * 128:(c + 1) * 128], identity=ident[:, :]), r=[hnb, ident], w=[psT])
            k.op("act", lambda e: e.activation(out=hnT[:, :, :], in_=psT[:, :, :], func=AF.Copy), r=[psT], w=[hnT])
            for g4 in range(4):
                pm = psM.next()
                for j in range(4):
                    hp = g4 * 4 + j
                    for c in range(8):
                        k.op("pe", lambda e: e.matmul(pm[:, j * 128:(j + 1) * 128], lhsT=wq[:, c, hp * 128:(hp + 1) * 128], rhs=hnT[:, c, :], start=(c == 0), stop=(c == 7)), r=[wq, hnT], w=[pm])
                k.op("act", lambda e: e.activation(out=qT[:, g4 * 4:(g4 + 1) * 4, :].rearrange("p a b -> p (a b)"), in_=pm[:, :], func=AF.Copy), r=[pm], w=[qT])
            for g4 in range(4):
                pm = psM.next()
                for j in range(4):
                    hp = g4 * 4 + j
                    k.op("pe", lambda e: e.matmul(pm[:, j * 128:(j + 1) * 128], lhsT=qT[:, hp, :], rhs=keysT[:, hp, :], start=True, stop=True), r=[qT, keysT], w=[pm])
                k.op("act", lambda e: e.activation(out=sc[:, g4 * 4:(g4 + 1) * 4, :].rearrange("p a b -> p (a b)"), in_=pm[:, :], func=AF.Copy), r=[pm], w=[sc])
            for hp in range(16):
                top16(k, sc, sc[:, hp, :], wk, wk[:, 0:128], tv, tv[:, hp, 0:8], tv[:, hp, 8:16], ti, ti[:, hp, 0:8], ti[:, hp, 8:16])
            k.op("dve", lambda e: e.tensor_copy(out=tif[:, :, :], in_=ti[:, :, :]), r=[ti], w=[tif])
            tv4 = tv[:, :, :].rearrange("p (h two) k -> p h two k", two=2)
            tif4 = tif[:, :, :].rearrange("p (h two) k -> p h two k", two=2)
            cand4 = cand[:, :, :].rearrange("p h (a b) -> p h a b", a=16)
            k.op("dve", lambda e: e.tensor_tensor(out=cand4, in0=tv4[:, :, 0, :].unsqueeze(3).to_broadcast([128, 8, 16, 16]), in1=tv4[:, :, 1, :].unsqueeze(2).to_broadcast([128, 8, 16, 16]), op=ALU.add), r=[tv], w=[cand])
            for h in range(8):
                top16(k, cand, cand[:, h, :], wk, wk[:, 0:256], bv, bv[:, h, 0:8], bv[:, h, 8:16], bp, bp[:, h, 0:8], bp[:, h, 8:16])
            k.op("dve", lambda e: e.tensor_copy(out=pbf[:, :, :], in_=bp[:, :, :]), r=[bp], w=[pbf])
            k.op("dve", lambda e: e.tensor_scalar(out=paf[:, :, :], in0=pbf[:, :, :], scalar1=0.0625, scalar2=None, op0=ALU.mult), r=[pbf], w=[paf])
            k.op("dve", lambda e: e.tensor_copy(out=pa_i[:, :, :], in_=paf[:, :, :]), r=[paf], w=[pa_i])
            k.op("dve", lambda e: e.tensor_copy(out=paf[:, :, :], in_=pa_i[:, :, :]), r=[pa_i], w=[paf])
            k.op("dve", lambda e: e.scalar_tensor_tensor(out=i1[:, :, :], in0=paf[:, :, :], scalar=16.0, in1=pbf[:, :, :], op0=ALU.mult, op1=ALU.is_gt), r=[paf, pbf], w=[i1])
            k.op("dve", lambda e: e.tensor_tensor(out=paf[:, :, :], in0=paf[:, :, :], in1=i1[:, :, :], op=ALU.subtract), r=[paf, i1], w=[paf])
            k.op("dve", lambda e: e.scalar_tensor_tensor(out=pbf[:, :, :], in0=paf[:, :, :], scalar=-16.0, in1=pbf[:, :, :], op0=ALU.mult, op1=ALU.add), r=[paf, pbf], w=[pbf])
            io4 = iota[:, 0:16].unsqueeze(1).unsqueeze(1).to_broadcast([128, 8, 16, 16])
            for (pf, half, dst) in ((paf, 0, i0), (pbf, 1, i1)):
                k.op("dve", lambda e: e.tensor_tensor(out=eq[:, :, :, :], in0=io4, in1=pf[:, :, :].unsqueeze(3).to_broadcast([128, 8, 16, 16]), op=ALU.is_equal), r=[iota, pf], w=[eq])
                k.op("dve", lambda e: e.tensor_tensor(out=eq[:, :, :, :], in0=eq[:, :, :, :], in1=tif4[:, :, half, :].unsqueeze(2).to_broadcast([128, 8, 16, 16]), op=ALU.mult), r=[eq, tif], w=[eq])
                k.op("dve", lambda e: e.tensor_reduce(out=dst[:, :, :], in_=eq[:, :, :, :], axis=AX.X, op=ALU.add), r=[eq], w=[dst])
            k.op("dve", lambda e: e.scalar_tensor_tensor(out=i0[:, :, :], in0=i0[:, :, :], scalar=128.0, in1=i1[:, :, :], op0=ALU.mult, op1=ALU.add), r=[i0, i1], w=[i0])
            if l > 0:
                k.op("dve", lambda e: e.tensor_scalar(out=i0[:, :, :], in0=i0[:, :, :], scalar1=float(l * 16384), scalar2=None, op0=ALU.add), r=[i0], w=[i0])
            k.op("dve", lambda e: e.tensor_copy(out=idx[:, :], in_=i0[:, :, :].rearrange("p a b -> p (a b)")), r=[i0], w=[idx])
            k.op("dve", lambda e: e.tensor_tensor(out=gw[:, :, :], in0=bv[:, :, :], in1=bv[:, :, 0:1].to_broadcast([128, 8, 16]), op=ALU.subtract), r=[bv], w=[gw])
            k.op("act", lambda e: e.activation(out=gw[:, :, :], in_=gw[:, :, :], func=AF.Exp), r=[gw], w=[gw])
            k.op("dve", lambda e: e.tensor_reduce(out=st[:, 0:8], in_=gw[:, :, :], axis=AX.X, op=ALU.add), r=[gw], w=[st])
            k.op("dve", lambda e: e.reciprocal(out=st[:, 0:8], in_=st[:, 0:8]), r=[st], w=[st])
            k.op("dve", lambda e: e.tensor_tensor(out=gw[:, :, :], in0=gw[:, :, :], in1=st[:, 0:8].unsqueeze(2).to_broadcast([128, 8, 16]), op=ALU.mult), r=[gw, st], w=[gw])
            for s in range(128):
                gb = gbuf.next()
                k.dma("pool", gb, gb[:, :], W["peer_u"], W["peer_u"].rearrange("l e d -> (l e) d"), idx=(idx, idx[:, s:s + 1]))
                k.op("dve", lambda e: e.scalar_tensor_tensor(out=junk[:, :], in0=gb[:, :], scalar=1.0, in1=hn[:, :], op0=ALU.mult, op1=ALU.mult, accum_out=hd[:, s:s + 1]), r=[gb, hn], w=[junk, hd])
            gelu_ops(k, hd, gx, gtmp, wgt, 128)
            k.op("dve", lambda e: e.tensor_tensor(out=wgt[:, :], in0=wgt[:, :], in1=gw[:, :, :].rearrange("p a b -> p (a b)"), op=ALU.mult), r=[wgt, gw], w=[wgt])
            for s in range(128):
                gb = gbuf.next()
                k.dma("pool", gb, gb[:, :], W["peer_v"], W["peer_v"].rearrange("l e d -> (l e) d"), idx=(idx, idx[:, s:s + 1]))
                if s == 0:
                    k.op("dve", lambda e: e.tensor_scalar(out=acc[:, :], in0=gb[:, :], scalar1=wgt[:, 0:1], scalar2=None, op0=ALU.mult), r=[gb, wgt], w=[acc])
                else:
                    k.op("dve", lambda e: e.scalar_tensor_tensor(out=acc[:, :], in0=gb[:, :], scalar=wgt[:, s:s + 1], in1=acc[:, :], op0=ALU.mult, op1=ALU.add), r=[gb, wgt, acc], w=[acc])
            y = yb.next()
            k.op("dve", lambda e: e.tensor_tensor(out=y[:, :], in0=acc[:, :], in1=hh[:, :], op=ALU.add), r=[acc, hh], w=[y])
            k.dma("sp", xout, xout[t0:t0 + 128, :], y, y[:, :])
        k.end_pass()


_CACHE = {}
FULL_SEQS = [16384, 4096, 4096]


def kernel(**inp):
    L = 2
    xp = np.asarray(inp["x_prompt"], np.float32)
    xs = np.asarray(inp["x_sample"], np.float32)
    cfg = make_cfg(FULL_SEQS, L)
    if "nc" not in _CACHE:
        _CACHE["nc"] = build(cfg)
    nc, kb = _CACHE["nc"]
    Wd = host_weights(inp, L)
    base = {n: Wd[n] for n in kb.used_inputs}
    in_maps = []
    for c in range(8):
        g = c % 2
        xg = np.concatenate([xp[g], xs[2 * g], xs[2 * g + 1]], axis=0)
        m = dict(base)
        m["x"] = np.ascontiguousarray(xg)
        in_maps.append(m)
    res = run_bass_kernel_spmd(nc, in_maps, core_ids=list(range(8)))
    yp = np.zeros_like(xp)
    ys = np.zeros_like(xs)
    for g in range(2):
        y = np.asarray(res.results[g]["y"], np.float32)
        yp[g] = y[0:16384]
        ys[2 * g] = y[16384:16384 + 4096]
        ys[2 * g + 1] = y[16384 + 4096:]
    return (yp, ys)
```
